# Optimizing a Trainium2 kernel written in Bass

```python
import jax, jax.numpy as jnp
from jax import lax
import numpy as np

D_MODEL = 1024
BATCH = 32
SEQ = 2048
DEPTH = 2

CHUNK = 64
N_BRANCH = 4
D_MIX = 512
LRU_BLOCKS = 8
LRU_BLOCK = D_MIX // LRU_BLOCKS
LRU_CONV = 4
LRU_C = 8.0
SCONV_WIDTH = 3
RWKV_HEAD = 64
RWKV_HEADS = D_MIX // RWKV_HEAD
DECAY_LORA = 64
ICLR_LORA = 64
GATE_LORA = 128
GN_EPS = RWKV_HEAD * 1e-5
ATT_HEAD = 64
ATT_HEADS = D_MIX // ATT_HEAD
LEFT_CHUNKS = 8
BAND = (LEFT_CHUNKS + 1) * CHUNK
REL_CLIP = 128
NEG_INF = -1e30
D_FF = 4 * D_MODEL
D_PLE = 256
ALPHA = (2 * DEPTH) ** 0.25
BETA = (8 * DEPTH) ** -0.25
LN_EPS = 1e-5
COLS_A = 2 * D_MIX
COLS_B = 3 * D_MIX
COLS_C = 3 * D_MIX + DECAY_LORA + ICLR_LORA + GATE_LORA
COLS_D = 3 * D_MIX
IN_COLS = COLS_A + COLS_B + COLS_C + COLS_D
IN_SPLITS = (COLS_A, COLS_A + COLS_B, COLS_A + COLS_B + COLS_C)
RWKV_SPLITS = (D_MIX, 2 * D_MIX, 3 * D_MIX, 3 * D_MIX + DECAY_LORA, 3 * D_MIX + DECAY_LORA + ICLR_LORA)

kernel_name = 'hybrid_gated_streaming_encoder'


def layer_norm(x, g, b):
    xf = x.astype(jnp.float32)
    mu = xf.mean(-1, keepdims=True)
    var = jnp.square(xf - mu).mean(-1, keepdims=True)
    return ((xf - mu) * lax.rsqrt(var + LN_EPS) * g + b).astype(x.dtype)


def causal_dwconv(x, w):
    k_width, chans = w.shape
    return lax.conv_general_dilated(x, w[:, None, :], window_strides=(1,), padding=[(k_width - 1, 0)],
                                    dimension_numbers=('NWC', 'WIO', 'NWC'), feature_group_count=chans)


def token_shift(z):
    return jnp.pad(z, ((0, 0), (1, 0), (0, 0)))[:, :-1]


def _linear_combine(left, right):
    a_l, b_l = left
    a_r, b_r = right
    return a_l * a_r, a_r * b_l + b_r


def rglru_branch(xa, ya, conv_w, conv_b, wr, br, wi, bi, lam):
    bsz, seq, _ = xa.shape
    xc = causal_dwconv(xa, conv_w) + conv_b
    xg = xc.reshape(bsz, seq, LRU_BLOCKS, LRU_BLOCK)
    r = jax.nn.sigmoid(jnp.einsum('bsgi,gij->bsgj', xg, wr).reshape(bsz, seq, D_MIX) + br)
    i = jax.nn.sigmoid(jnp.einsum('bsgi,gij->bsgj', xg, wi).reshape(bsz, seq, D_MIX) + bi)
    log_a = -LRU_C * r.astype(jnp.float32) * jax.nn.softplus(-lam.astype(jnp.float32))
    a = jnp.exp(log_a)
    u = (i * xc).astype(jnp.float32) * jnp.sqrt(-jnp.expm1(2.0 * log_a))
    _, h = lax.associative_scan(_linear_combine, (a, u), axis=1)
    return h.astype(xa.dtype) * jax.nn.gelu(ya, approximate=True)


def short_conv_branch(b_gate, c_gate, xh, conv_w):
    return b_gate * causal_dwconv(c_gate * xh, conv_w)


def rwkv7_branch(z, mu, w0, w2, a0, a2, g2, k_k, k_a, r_k, gn_g, gn_b):
    bsz, seq, _ = z.shape
    f32 = jnp.float32
    z = z + (token_shift(z) - z) * mu
    r, k, v, wd, ad, gd = jnp.split(z, RWKV_SPLITS, axis=-1)
    w_log = -jax.nn.softplus(-(w0 + jnp.tanh(wd) @ w2).astype(f32)) - 0.5
    decay = jnp.exp(-jnp.exp(w_log))
    a = jax.nn.sigmoid(a0 + ad @ a2)
    g = jax.nn.sigmoid(gd) @ g2

    def heads(t):
        return t.astype(f32).reshape(bsz, seq, RWKV_HEADS, RWKV_HEAD)

    kk = heads(k * k_k)
    kk = kk / jnp.maximum(jnp.sqrt(jnp.sum(kk * kk, axis=-1, keepdims=True)), 1e-12)
    k = heads(k * (1.0 + (a - 1.0) * k_a))
    r, v, a, decay = heads(r), heads(v), heads(a), heads(decay)

    def time_major(t):
        return jnp.moveaxis(t, 1, 0)

    def step(state, inp):
        r_t, w_t, k_t, v_t, a_t, b_t = inp
        sa = jnp.einsum('bhvk,bhk->bhv', state, a_t)
        state = (state * w_t[:, :, None, :] + sa[..., None] * b_t[:, :, None, :]
                 + v_t[..., None] * k_t[:, :, None, :])
        return state, jnp.einsum('bhvk,bhk->bhv', state, r_t)

    state0 = jnp.zeros((bsz, RWKV_HEADS, RWKV_HEAD, RWKV_HEAD), f32)
    _, o = lax.scan(step, state0, (time_major(r), time_major(decay), time_major(k), time_major(v),
                                   time_major(-kk), time_major(kk * a)))
    o = jnp.moveaxis(o, 0, 1)
    mean = o.mean(-1, keepdims=True)
    var = jnp.square(o - mean).mean(-1, keepdims=True)
    o = ((o - mean) * lax.rsqrt(var + GN_EPS) * gn_g.reshape(RWKV_HEADS, RWKV_HEAD)
         + gn_b.reshape(RWKV_HEADS, RWKV_HEAD))
    o = o + jnp.sum(r * k * r_k, axis=-1, keepdims=True) * v
    return (o.reshape(bsz, seq, D_MIX) * g).astype(z.dtype)


def chunk_attention(q, k, v, rel_bias):
    bsz, seq, _ = q.shape
    n_chunks = seq // CHUNK

    def heads(t):
        return t.reshape(bsz, seq, ATT_HEADS, ATT_HEAD).transpose(0, 2, 1, 3)

    q = heads(q) * (ATT_HEAD ** -0.5)
    pad = ((0, 0), (0, 0), (BAND - CHUNK, 0), (0, 0))
    k = jnp.pad(heads(k), pad)
    v = jnp.pad(heads(v), pad)
    rel = (BAND - CHUNK) + np.arange(CHUNK)[:, None] - np.arange(BAND)[None, :]
    bias = rel_bias[:, np.clip(rel, -REL_CLIP, REL_CLIP) + REL_CLIP].astype(jnp.float32)
    band_offsets = jnp.arange(BAND)

    def one_chunk(c):
        start = c * CHUNK
        qc = lax.dynamic_slice_in_dim(q, start, CHUNK, axis=2)
        kc = lax.dynamic_slice_in_dim(k, start, BAND, axis=2)
        vc = lax.dynamic_slice_in_dim(v, start, BAND, axis=2)
        s = jnp.einsum('bhqd,bhkd->bhqk', qc, kc).astype(jnp.float32) + bias
        valid = (start - (BAND - CHUNK) + band_offsets) >= 0
        s = jnp.where(valid, s, NEG_INF)
        pr = jax.nn.softmax(s, axis=-1).astype(vc.dtype)
        return jnp.einsum('bhqk,bhkd->bhqd', pr, vc)

    o = lax.map(one_chunk, jnp.arange(n_chunks))
    return o.transpose(1, 0, 3, 2, 4).reshape(bsz, seq, D_MIX)


def setup_inputs(seed: int = 0) -> dict:
    key = jax.random.key(seed)
    keys = iter(jax.random.split(key, 48))
    f32 = jnp.float32
    L = DEPTH

    def nrm(shape, scale):
        return jax.random.normal(next(keys), shape, f32) * scale

    x = nrm((BATCH, SEQ, D_MODEL), 1.0)
    p = nrm((DEPTH, BATCH, SEQ, D_PLE), 1.0)
    w_in = nrm((L, D_MODEL, IN_COLS), D_MODEL ** -0.5)
    lru_conv_w = nrm((L, LRU_CONV, D_MIX), LRU_CONV ** -0.5)
    lru_conv_b = nrm((L, D_MIX), 0.01)
    lru_wr = nrm((L, LRU_BLOCKS, LRU_BLOCK, LRU_BLOCK), LRU_BLOCK ** -0.5)
    lru_br = nrm((L, D_MIX), 0.01)
    lru_wi = nrm((L, LRU_BLOCKS, LRU_BLOCK, LRU_BLOCK), LRU_BLOCK ** -0.5)
    lru_bi = nrm((L, D_MIX), 0.01)
    a_pow = jax.random.uniform(next(keys), (L, D_MIX), f32, 0.9, 0.999)
    s_lam = a_pow ** (1.0 / LRU_C)
    lru_lambda = jnp.log(s_lam) - jnp.log1p(-s_lam)
    sconv_w = nrm((L, SCONV_WIDTH, D_MIX), SCONV_WIDTH ** -0.5)
    rwkv_mu = jax.random.uniform(next(keys), (L, COLS_C), f32)
    rwkv_w0 = jax.random.uniform(next(keys), (L, D_MIX), f32, -6.0, -1.0)
    rwkv_w2 = nrm((L, DECAY_LORA, D_MIX), 0.1)
    rwkv_a0 = nrm((L, D_MIX), 0.1)
    rwkv_a2 = nrm((L, ICLR_LORA, D_MIX), ICLR_LORA ** -0.5)
    rwkv_g2 = nrm((L, GATE_LORA, D_MIX), GATE_LORA ** -0.5)
    rwkv_k_k = 0.85 + nrm((L, D_MIX), 0.02)
    rwkv_k_a = 1.0 + nrm((L, D_MIX), 0.02)
    rwkv_r_k = nrm((L, RWKV_HEADS, RWKV_HEAD), 0.1)
    rwkv_gn_g = 1.0 + nrm((L, D_MIX), 0.02)
    rwkv_gn_b = nrm((L, D_MIX), 0.01)
    rel_bias = nrm((ATT_HEADS, 2 * REL_CLIP + 1), 0.5)
    w_branch = nrm((L, N_BRANCH, D_MIX, D_MODEL), D_MIX ** -0.5 * BETA)
    w_gate = nrm((L, N_BRANCH, D_MODEL, D_MODEL), D_MODEL ** -0.5)
    b_gate = nrm((L, N_BRANCH, D_MODEL), 0.01)
    w_out = nrm((L, D_MODEL, D_MODEL), D_MODEL ** -0.5 * BETA)
    ln1_g = 1.0 + nrm((L, D_MODEL), 0.02)
    ln1_b = nrm((L, D_MODEL), 0.01)
    w_ff1 = nrm((L, D_MODEL, D_FF), D_MODEL ** -0.5)
    w_ff2 = nrm((L, D_FF, D_MODEL), D_FF ** -0.5 * BETA)
    w_ple = nrm((L, D_PLE, D_MODEL), D_PLE ** -0.5)
    w_ple_gate = nrm((L, D_MODEL, D_MODEL), D_MODEL ** -0.5)
    b_ple_gate = nrm((L, D_MODEL), 0.01)
    ln2_g = 1.0 + nrm((L, D_MODEL), 0.02)
    ln2_b = nrm((L, D_MODEL), 0.01)
    return {'x': x, 'p': p, 'w_in': w_in,
            'lru_conv_w': lru_conv_w, 'lru_conv_b': lru_conv_b, 'lru_wr': lru_wr, 'lru_br': lru_br,
            'lru_wi': lru_wi, 'lru_bi': lru_bi, 'lru_lambda': lru_lambda,
            'sconv_w': sconv_w,
            'rwkv_mu': rwkv_mu, 'rwkv_w0': rwkv_w0, 'rwkv_w2': rwkv_w2, 'rwkv_a0': rwkv_a0,
            'rwkv_a2': rwkv_a2, 'rwkv_g2': rwkv_g2, 'rwkv_k_k': rwkv_k_k, 'rwkv_k_a': rwkv_k_a,
            'rwkv_r_k': rwkv_r_k, 'rwkv_gn_g': rwkv_gn_g, 'rwkv_gn_b': rwkv_gn_b,
            'rel_bias': rel_bias, 'w_branch': w_branch, 'w_gate': w_gate, 'b_gate': b_gate,
            'w_out': w_out, 'ln1_g': ln1_g, 'ln1_b': ln1_b, 'w_ff1': w_ff1, 'w_ff2': w_ff2,
            'w_ple': w_ple, 'w_ple_gate': w_ple_gate, 'b_ple_gate': b_ple_gate,
            'ln2_g': ln2_g, 'ln2_b': ln2_b}


def reference(x, p, w_in, lru_conv_w, lru_conv_b, lru_wr, lru_br, lru_wi, lru_bi, lru_lambda,
              sconv_w, rwkv_mu, rwkv_w0, rwkv_w2, rwkv_a0, rwkv_a2, rwkv_g2, rwkv_k_k, rwkv_k_a,
              rwkv_r_k, rwkv_gn_g, rwkv_gn_b, rel_bias, w_branch, w_gate, b_gate, w_out, ln1_g, ln1_b,
              w_ff1, w_ff2, w_ple, w_ple_gate, b_ple_gate, ln2_g, ln2_b):
    for l in range(DEPTH):
        h = x @ w_in[l]
        h_a, h_b, h_c, h_d = jnp.split(h, IN_SPLITS, axis=-1)
        xa, ya = jnp.split(h_a, 2, axis=-1)
        y_a = rglru_branch(xa, ya, lru_conv_w[l], lru_conv_b[l], lru_wr[l], lru_br[l],
                           lru_wi[l], lru_bi[l], lru_lambda[l])
        b_g, c_g, xh = jnp.split(h_b, 3, axis=-1)
        y_b = short_conv_branch(b_g, c_g, xh, sconv_w[l])
        y_c = rwkv7_branch(h_c, rwkv_mu[l], rwkv_w0[l], rwkv_w2[l], rwkv_a0[l], rwkv_a2[l], rwkv_g2[l],
                           rwkv_k_k[l], rwkv_k_a[l], rwkv_r_k[l], rwkv_gn_g[l], rwkv_gn_b[l])
        q, k, v = jnp.split(h_d, 3, axis=-1)
        y_d = chunk_attention(q, k, v, rel_bias)
        branches = (y_a, y_b, y_c, y_d)
        merged = jax.nn.sigmoid(x @ w_gate[l, 0] + b_gate[l, 0]) * (branches[0] @ w_branch[l, 0])
        for n in range(1, N_BRANCH):
            merged = merged + jax.nn.sigmoid(x @ w_gate[l, n] + b_gate[l, n]) * (branches[n] @ w_branch[l, n])
        x = layer_norm(ALPHA * x + merged @ w_out[l], ln1_g[l], ln1_b[l])
        ff = jnp.square(jax.nn.relu(x @ w_ff1[l])) @ w_ff2[l]
        ple = (p[l] @ w_ple[l]) * jax.nn.sigmoid(x @ w_ple_gate[l] + b_ple_gate[l])
        x = layer_norm(ALPHA * x + ff + ple, ln2_g[l], ln2_b[l])
    return x
```

```python
import numpy as np
from contextlib import ExitStack
import concourse.bass as bass
import concourse.mybir as mybir
from concourse.bass_utils import run_bass_kernel_spmd

F32 = mybir.dt.float32
F32R = mybir.dt.float32r
BF16 = mybir.dt.bfloat16
U8 = mybir.dt.uint8
ALU = mybir.AluOpType
AF = mybir.ActivationFunctionType
AX = mybir.AxisListType

S = 2048
DM = 1024
NCOL = 5888
ALPHA = 4.0 ** 0.25
C0 = float(np.exp(-0.5))
GN_EPS = 64 * 1e-5
LN_EPS = 1e-5

PV = {}
_o = 0
for _n, _w in (("cw", 16), ("cb", 4), ("br", 4), ("bi", 4), ("lam", 4), ("sw", 12), ("mu", 14), ("w0", 4),
               ("a0", 4), ("kk", 4), ("ka", 4), ("rk", 4), ("gng", 4), ("gnb", 4), ("bg", 32), ("l1g", 8),
               ("l1b", 8), ("bpg", 8), ("l2g", 8), ("l2b", 8), ("omka", 4), ("clam", 4)):
    PV[_n] = _o
    _o += _w
NPV = _o


class Tok:
    __slots__ = ("w", "r")

    def __init__(self):
        self.w = None
        self.r = []


class Tile:
    __slots__ = ("ap", "tok")

    def __init__(self, ap, tok=None):
        self.ap = ap
        self.tok = tok or Tok()


class KB:
    def __init__(self, nc, es):
        self.nc = nc
        self.es = es
        self.eng = {"pe": nc.tensor, "act": nc.scalar, "dve": nc.vector, "pool": nc.gpsimd, "sp": nc.sync}
        self.sems = {}
        self.cnt = {}
        self.seen = {e: {} for e in self.eng}
        for e in self.eng:
            self._sem("c_" + e)
        self.ninstr = 0

    def _sem(self, name):
        if name not in self.sems:
            self.sems[name] = self.es.enter_context(self.nc.semaphore(name))
            self.cnt[name] = 0
        return self.sems[name]

    def _deps(self, reads, writes):
        deps = []
        for t in reads:
            if t.w is not None:
                deps.append(t.w)
        for t in writes:
            if t.w is not None:
                deps.append(t.w)
            deps.extend(t.r)
        return deps

    def _wait(self, eng, deps):
        best = {}
        for (s, v) in deps:
            if eng == "pe" and s == "c_pe":
                continue
            if self.seen[eng].get(s, 0) >= v:
                continue
            if best.get(s, 0) < v:
                best[s] = v
        for s, v in best.items():
            self.eng[eng].wait_ge(self.sems[s], v)
            self.seen[eng][s] = v
            self.ninstr += 1

    def _reg(self, ev, reads, writes):
        for t in reads:
            t.r = [e for e in t.r if e[0] != ev[0]]
            t.r.append(ev)
        for t in writes:
            t.w = ev
            t.r = []

    def op(self, eng, fn, reads=(), writes=(), signal=True):
        self._wait(eng, self._deps(reads, writes))
        ins = fn(self.eng[eng])
        self.ninstr += 1
        s = "c_" + eng
        if signal:
            self.cnt[s] += 1
            ins.then_inc(self.sems[s], 1)
            ev = (s, self.cnt[s])
        else:
            ev = (s, self.cnt[s] + 1)
        self._reg(ev, reads, writes)

    def dma(self, q, out, in_, reads=(), writes=(), sem="dma"):
        sem = sem + "@" + q
        self._sem(sem)
        self._wait(q, self._deps(reads, writes))
        ins = self.eng[q].dma_start(out=out, in_=in_)
        self.ninstr += 1
        self.cnt[sem] += 16
        ins.then_inc(self.sems[sem], 16)
        self._reg((sem, self.cnt[sem]), reads, writes)

    def barrier(self, engines=None):
        for e in (engines or self.eng):
            for s, v in self.cnt.items():
                if v > 0 and self.seen[e].get(s, 0) < v:
                    self.eng[e].wait_ge(self.sems[s], v)
                    self.seen[e][s] = v
                    self.ninstr += 1


class Arena:
    def __init__(self, ap, nbytes):
        self.ap = ap
        self.nbytes = nbytes
        self.off = 0
        self.base = 0

    def alloc(self, shape, dt, parts=128, at=None):
        if at is not None:
            save = self.off
            self.off = at
            t = self.alloc(shape, dt, parts)
            self.off = save
            return t
        self.last = self.off
        esz = 4 if dt == F32 else 2
        n = int(np.prod(shape))
        nb = (n * esz + 31) // 32 * 32
        assert self.off + nb <= self.nbytes, ("SBUF arena overflow", self.off, nb)
        a = self.ap[0:parts, self.off:self.off + nb]
        if nb != n * esz:
            a = self.ap[0:parts, self.off:self.off + n * esz]
        a = a.bitcast(dt)
        self.off += nb
        if len(shape) == 2:
            a = a.rearrange("p (a b) -> p a b", a=shape[0])
        elif len(shape) == 3:
            a = a.rearrange("p (a b c) -> p a b c", a=shape[0], b=shape[1])
        elif len(shape) == 4:
            a = a.rearrange("p (a b c d) -> p a b c d", a=shape[0], b=shape[1], c=shape[2])
        return Tile(a)

    def mark(self):
        self.base = self.off

    def reset(self):
        self.off = self.base


def build_program(NB, stop_after=None, dbg=False, nlayers=2):
    T = NB * S
    nc = bass.Bass("TRN2", target_bir_lowering=False)
    es = ExitStack()
    K = KB(nc, es)

    def din(name, shape):
        return nc.dram_tensor(name, list(shape), F32, kind="ExternalInput").ap()

    def dscr(name, shape, dt=F32):
        return nc.dram_tensor(name, list(shape), dt, kind=("ExternalOutput" if dbg else "Internal")).ap()

    xT = din("xT", [DM, T])
    pT = din("pT", [2, 256, T])
    w_in = din("w_in", [2, DM, NCOL])
    w_gate = din("w_gate", [2, DM, 4096])
    w_branch = din("w_branch", [2, 4, 512, DM])
    w_out = din("w_out", [2, DM, DM])
    w_ff1 = din("w_ff1", [2, DM, 4096])
    w_ff2 = din("w_ff2", [2, 4096, DM])
    w_ple = din("w_ple", [2, 256, DM])
    w_pg = din("w_pg", [2, DM, DM])
    lru_wr = din("lru_wr", [2, 8, 64, 64])
    lru_wi = din("lru_wi", [2, 8, 64, 64])
    rw2 = din("rw2", [2, 64, 512])
    ra2 = din("ra2", [2, 64, 512])
    rg2 = din("rg2", [2, 128, 512])
    pvec = din("pvec", [2, 128, NPV])
    muv = din("muv", [2, 64, 512])
    c_ident = din("c_ident", [128, 128])
    c_onesbd = din("c_onesbd", [128, 128])
    c_mask1 = din("c_mask1", [64, 128])
    c_masksl = din("c_masksl", [64, 64])
    c_rmask = din("c_rmask", [128, 512])
    c_bias = din("c_bias", [128, 8 * 5 * 128])
    outT = nc.dram_tensor("outT", [DM, T], F32, kind="ExternalOutput").ap()

    hFM = dscr("hFM", [NCOL, T])
    vTM = dscr("vTM", [T, 1024])
    yall = dscr("yall", [2048, T], BF16)
    x1d = dscr("x1d", [DM, T])
    xLd = dscr("xLd", [DM, T])
    based = dscr("based", [DM, T])
    mrgd = dscr("mrgd", [DM, T], BF16)

    arena_t = es.enter_context(nc.sbuf_tensor("arena", [128, 207 * 1024], U8))
    A = Arena(arena_t, 207 * 1024)
    psum_t = es.enter_context(nc.psum_tensor("psum", [128, 4096], F32))
    banks = [Tile(psum_t[:, i * 512:(i + 1) * 512]) for i in range(8)]
    bank_i = [0]

    def bank():
        b = banks[bank_i[0] % 8]
        bank_i[0] += 1
        return b

    pv = [A.alloc([NPV], F32) for _ in range(2)]
    ident = A.alloc([128], F32)
    identb = A.alloc([128], BF16)
    onesbd = A.alloc([128], F32)
    onesb = A.alloc([64], BF16)
    onesf = A.alloc([128], F32)
    mask1 = A.alloc([128], F32)
    masksl = A.alloc([64], F32)
    rmask = A.alloc([512], F32)
    ctok = Tok()
    for l in range(2):
        K.dma("sp", pv[l].ap, pvec[l], writes=[ctok], sem="ld_c")
    K.dma("sp", ident.ap, c_ident[:, :], writes=[ctok], sem="ld_c")
    K.dma("pool", identb.ap, c_ident[:, :], writes=[ctok], sem="ld_c")
    K.dma("sp", onesbd.ap, c_onesbd[:, :], writes=[ctok], sem="ld_c")
    K.dma("sp", mask1.ap[0:64, :], c_mask1[:, :], writes=[ctok], sem="ld_c")
    K.dma("sp", mask1.ap[64:128, :], c_mask1[:, :], writes=[ctok], sem="ld_c")
    K.dma("sp", masksl.ap[0:64, :], c_masksl[:, :], writes=[ctok], sem="ld_c")
    K.dma("sp", rmask.ap, c_rmask[:, :], writes=[ctok], sem="ld_c")
    K.op("dve", lambda e: e.memset(onesb.ap, 1.0), writes=[ctok])
    K.op("dve", lambda e: e.memset(onesf.ap, 1.0), writes=[ctok])
    for l in range(2):
        p = pv[l].ap
        K.op("dve", lambda e, p=p: e.tensor_scalar(p[:, PV["omka"]:PV["omka"] + 4], p[:, PV["ka"]:PV["ka"] + 4],
                                                   -1.0, 1.0, ALU.mult, ALU.add), reads=[ctok], writes=[ctok])
        K.op("act", lambda e, p=p: e.activation(out=p[:, PV["clam"]:PV["clam"] + 4], in_=p[:, PV["lam"]:PV["lam"] + 4],
                                                func=AF.Exp, scale=-1.0), reads=[ctok], writes=[ctok])
        K.op("act", lambda e, p=p: e.activation(out=p[:, PV["clam"]:PV["clam"] + 4], in_=p[:, PV["clam"]:PV["clam"] + 4],
                                                func=AF.Ln, bias=1.0), reads=[ctok], writes=[ctok])
        K.op("act", lambda e, p=p: e.mul(p[:, PV["clam"]:PV["clam"] + 4], p[:, PV["clam"]:PV["clam"] + 4], -8.0),
             reads=[ctok], writes=[ctok])
    K.barrier()
    A.mark()

    def pcol(l, name, j):
        c = PV[name] + j
        return pv[l].ap[:, c:c + 1]

    evi = [0]

    def evac_eng():
        evi[0] += 1
        return "act" if evi[0] % 2 else "dve"

    def copy_op(eng, out, in_):
        if eng == "act":
            return lambda e: e.copy(out, in_)
        return lambda e: e.tensor_copy(out, in_)

    def stage_inproj(l, xsrc):
        A.reset()
        wsb = A.alloc([8, NCOL], BF16)
        for kc in range(8):
            K.dma("pool", wsb.ap[:, kc, :], w_in[l, kc * 128:(kc + 1) * 128, :], writes=[wsb.tok], sem="ld_w")
        xb = [A.alloc([8, 512], BF16) for _ in range(2)]
        stg = [A.alloc([512], F32) for _ in range(6)]
        si = 0
        xs3 = xsrc.rearrange("(kc p) t -> p kc t", p=128)
        import os
        _ntb = int(os.environ.get("KDBG_TB", T // 512)); _noc = int(os.environ.get("KDBG_OC", 46)); _ntm = int(os.environ.get("KDBG_TM", 4))
        for tb in range(_ntb):
            xs = xb[tb % 2]
            K.dma("pool", xs.ap, xs3[:, :, tb * 512:(tb + 1) * 512], writes=[xs.tok], sem="ld_x%d" % (tb % 2))
            for oc in range(_noc):
                if oc >= 42:
                    continue
                bk = bank()
                for kc in range(8):
                    K.op("pe", lambda e, kc=kc, oc=oc, bk=bk, xs=xs: e.matmul(
                        bk.ap, wsb.ap[:, kc, oc * 128:(oc + 1) * 128], xs.ap[:, kc, :], start=(kc == 0), stop=(kc == 7)),
                        reads=[wsb.tok, xs.tok], writes=[bk.tok], signal=(kc == 7))
                st = stg[si % 6]
                K.op(evac_eng(), copy_op("act" if evi[0] % 2 else "dve", st.ap, bk.ap), reads=[bk.tok], writes=[st.tok])
                K.dma("sp", hFM[oc * 128:(oc + 1) * 128, tb * 512:(tb + 1) * 512], st.ap, reads=[st.tok],
                      sem="st%d" % (si % 6))
                si += 1
            for (c0, d0) in ((3584, 0), (5376, 512)):
                for tt in range(_ntm):
                    bk = bank()
                    for kc in range(8):
                        K.op("pe", lambda e, kc=kc, bk=bk, xs=xs, tt=tt, c0=c0: e.matmul(
                            bk.ap, xs.ap[:, kc, tt * 128:(tt + 1) * 128], wsb.ap[:, kc, c0:c0 + 512],
                            start=(kc == 0), stop=(kc == 7)),
                            reads=[wsb.tok, xs.tok], writes=[bk.tok], signal=(kc == 7))
                    st = stg[si % 6]
                    K.op(evac_eng(), copy_op("act" if evi[0] % 2 else "dve", st.ap, bk.ap), reads=[bk.tok], writes=[st.tok])
                    r0 = tb * 512 + tt * 128
                    K.dma("sp", vTM[r0:r0 + 128, d0:d0 + 512], st.ap, reads=[st.tok], sem="st%d" % (si % 6))
                    si += 1
        K.barrier()

    def interleave(*gens):
        gens = [g for g in gens if g is not None]
        while gens:
            for g in list(gens):
                try:
                    next(g)
                except StopIteration:
                    gens.remove(g)

    def stage_rglru(l):
        A.reset()
        wr = [A.alloc([128], F32) for _ in range(4)]
        wi = [A.alloc([128], F32) for _ in range(4)]
        wtok = Tok()
        for cc in range(4):
            for w_t, src in ((wr[cc], lru_wr), (wi[cc], lru_wi)):
                K.op("dve", lambda e, w_t=w_t: e.memset(w_t.ap, 0.0), writes=[wtok])
                K.dma("sp", w_t.ap[0:64, 0:64], src[l, 2 * cc, :, :], writes=[wtok], sem="ld_w")
                K.dma("sp", w_t.ap[64:128, 64:128], src[l, 2 * cc + 1, :, :], writes=[wtok], sem="ld_w")
        nb = 2
        xa = [A.alloc([S + 3], F32) for _ in range(nb)]
        ya = [A.alloc([S], F32) for _ in range(nb)]
        for t_ in xa:
            K.op("dve", lambda e, t_=t_: e.memset(t_.ap[:, 0:3], 0.0), writes=[t_.tok])
        xc2 = [A.alloc([S], F32) for _ in range(2)]
        rr2 = [A.alloc([S], F32) for _ in range(2)]
        ii2 = [A.alloc([S], F32) for _ in range(2)]
        aa2 = [A.alloc([S], F32) for _ in range(2)]
        mm2 = [A.alloc([S], F32) for _ in range(2)]
        hh2 = [A.alloc([S], F32) for _ in range(2)]
        gg2 = [A.alloc([S], F32) for _ in range(2)]
        oo = [A.alloc([S], BF16) for _ in range(2)]
        def unit(it, cc, b):
            if True:
                x_ = xa[it % nb]
                y_ = ya[it % nb]
                o_ = oo[it % 2]
                xc, rr, ii, aa, mm, hh, gg = (xc2[it % 2], rr2[it % 2], ii2[it % 2], aa2[it % 2], mm2[it % 2], hh2[it % 2],
                                              gg2[it % 2])
                uu = ii
                K.dma("sp", x_.ap[:, 3:], hFM[cc * 128:(cc + 1) * 128, b * S:(b + 1) * S], writes=[x_.tok],
                      sem="ld_x%d" % (it % nb))
                K.dma("sp", y_.ap, hFM[512 + cc * 128:512 + (cc + 1) * 128, b * S:(b + 1) * S], writes=[y_.tok],
                      sem="ld_y%d" % (it % nb))
                cw = lambda k: pcol(l, "cw", cc * 4 + k)
                K.op("act", lambda e: e.activation(out=xc.ap, in_=x_.ap[:, 0:S], func=AF.Identity, scale=cw(0),
                                                   bias=pcol(l, "cb", cc)), reads=[x_.tok], writes=[xc.tok])
                for k in range(1, 4):
                    K.op("dve", lambda e, k=k: e.scalar_tensor_tensor(xc.ap, x_.ap[:, k:k + S], cw(k), xc.ap, ALU.mult, ALU.add),
                         reads=[x_.tok, xc.tok], writes=[xc.tok])
                yield
                for q in range(4):
                    sl = slice(q * 512, (q + 1) * 512)
                    yield
                    for (w_t, dst, bn) in ((wr[cc], rr, "br"), (wi[cc], ii, "bi")):
                        bk = bank()
                        K.op("pe", lambda e, bk=bk, w_t=w_t, sl=sl: e.matmul(bk.ap, w_t.ap, xc.ap[:, sl], start=True, stop=True),
                             reads=[wtok, xc.tok], writes=[bk.tok])
                        K.op("act", lambda e, bk=bk, dst=dst, sl=sl, bn=bn: e.activation(
                            out=dst.ap[:, sl], in_=bk.ap, func=AF.Sigmoid, bias=pcol(l, bn, cc)),
                            reads=[bk.tok], writes=[dst.tok])
                yield
                K.op("act", lambda e: e.activation(out=aa.ap, in_=rr.ap, func=AF.Exp, scale=pcol(l, "clam", cc)),
                     reads=[rr.tok], writes=[aa.tok])
                K.op("act", lambda e: e.activation(out=mm.ap, in_=aa.ap, func=AF.Square), reads=[aa.tok], writes=[mm.tok])
                K.op("act", lambda e: e.activation(out=mm.ap, in_=mm.ap, func=AF.Sqrt, scale=-1.0, bias=1.0),
                     reads=[mm.tok], writes=[mm.tok])
                yield
                K.op("dve", lambda e: e.tensor_tensor(uu.ap, ii.ap, xc.ap, ALU.mult), reads=[ii.tok, xc.tok], writes=[uu.tok])
                K.op("dve", lambda e: e.tensor_tensor(uu.ap, uu.ap, mm.ap, ALU.mult), reads=[uu.tok, mm.tok], writes=[uu.tok])
                K.op("dve", lambda e: e.tensor_tensor_scan(hh.ap, aa.ap, uu.ap, 0.0, ALU.mult, ALU.add),
                     reads=[aa.tok, uu.tok], writes=[hh.tok])
                yield
                K.op("act", lambda e: e.activation(out=gg.ap, in_=y_.ap, func=AF.Square), reads=[y_.tok], writes=[gg.tok])
                K.op("dve", lambda e: e.tensor_scalar(gg.ap, gg.ap, 0.044715, 1.0, ALU.mult, ALU.add),
                     reads=[gg.tok], writes=[gg.tok])
                K.op("dve", lambda e: e.tensor_tensor(gg.ap, gg.ap, y_.ap, ALU.mult), reads=[gg.tok, y_.tok], writes=[gg.tok])
                yield
                K.op("act", lambda e: e.activation(out=gg.ap, in_=gg.ap, func=AF.Sigmoid, scale=1.5957691216),
                     reads=[gg.tok], writes=[gg.tok])
                K.op("dve", lambda e: e.tensor_tensor(gg.ap, gg.ap, y_.ap, ALU.mult), reads=[gg.tok, y_.tok], writes=[gg.tok])
                K.op("dve", lambda e: e.tensor_tensor(o_.ap, hh.ap, gg.ap, ALU.mult), reads=[hh.tok, gg.tok], writes=[o_.tok])
                K.dma("sp", yall[cc * 128:(cc + 1) * 128, b * S:(b + 1) * S], o_.ap, reads=[o_.tok], sem="st%d" % (it % 2))
                yield

        units = [(cc, b) for cc in range(4) for b in range(NB)]
        for k in range(0, len(units), 2):
            interleave(unit(k, *units[k]), unit(k + 1, *units[k + 1]) if k + 1 < len(units) else None)
        K.barrier()

    def stage_sconv(l):
        A.reset()
        nb = 2
        bgt = [A.alloc([S], F32) for _ in range(nb)]
        cgt = [A.alloc([S], F32) for _ in range(nb)]
        xht = [A.alloc([S], F32) for _ in range(nb)]
        tt2 = [A.alloc([S + 2], F32) for _ in range(2)]
        for t_ in tt2:
            K.op("dve", lambda e: e.memset(t_.ap[:, 0:2], 0.0), writes=[t_.tok])
        yy2 = [A.alloc([S], F32) for _ in range(2)]
        oo = [A.alloc([S], BF16) for _ in range(2)]
        def unit(it, cc, b):
            if True:
                i_ = it % nb
                for (t_, r0, nm) in ((bgt[i_], 1024, "a"), (cgt[i_], 1536, "b"), (xht[i_], 2048, "c")):
                    K.dma("sp", t_.ap, hFM[r0 + cc * 128:r0 + (cc + 1) * 128, b * S:(b + 1) * S], writes=[t_.tok],
                          sem="ld_%s%d" % (nm, i_))
                o_ = oo[it % 2]
                tt_, yy = tt2[it % 2], yy2[it % 2]
                sw = lambda k: pcol(l, "sw", cc * 3 + k)
                K.op("dve", lambda e: e.tensor_tensor(tt_.ap[:, 2:], cgt[i_].ap, xht[i_].ap, ALU.mult),
                     reads=[cgt[i_].tok, xht[i_].tok], writes=[tt_.tok])
                yield
                K.op("act", lambda e: e.activation(out=yy.ap, in_=tt_.ap[:, 0:S], func=AF.Copy, scale=sw(0)),
                     reads=[tt_.tok], writes=[yy.tok])
                for k in (1, 2):
                    K.op("dve", lambda e, k=k: e.scalar_tensor_tensor(yy.ap, tt_.ap[:, k:k + S], sw(k), yy.ap, ALU.mult, ALU.add),
                         reads=[tt_.tok, yy.tok], writes=[yy.tok])
                K.op("dve", lambda e: e.tensor_tensor(o_.ap, yy.ap, bgt[i_].ap, ALU.mult), reads=[yy.tok, bgt[i_].tok],
                     writes=[o_.tok])
                yield
                K.dma("sp", yall[512 + cc * 128:512 + (cc + 1) * 128, b * S:(b + 1) * S], o_.ap, reads=[o_.tok],
                      sem="st%d" % (it % 2))
                yield

        units = [(cc, b) for cc in range(4) for b in range(NB)]
        for k in range(0, len(units), 2):
            interleave(unit(k, *units[k]), unit(k + 1, *units[k + 1]) if k + 1 < len(units) else None)
        K.barrier()

    def stage_attn(l):
        A.reset()
        bias = A.alloc([8, 5, 128], F32)
        K.dma("sp", bias.ap, c_bias.rearrange("p (h v k) -> p h v k", h=8, v=5), writes=[bias.tok], sem="ld_w")
        qb = A.alloc([4, S], BF16)
        kb_ = A.alloc([4, S], BF16)
        vb = A.alloc([16, 512], BF16)
        vaug = A.alloc([16, 8, 128], BF16)
        of = A.alloc([4, S], BF16)
        sT = [A.alloc([640], F32) for _ in range(2)]
        pTt = [A.alloc([640], BF16) for _ in range(2)]
        rd = [A.alloc([128], F32) for _ in range(2)]
        psS = [Tile(psum_t[:, 0:1024]), Tile(psum_t[:, 1024:2048])]
        psO = [Tile(psum_t[:, 2048:2560]), Tile(psum_t[:, 2560:3072])]
        K.op("pool", lambda e: e.memset(vaug.ap, 1.0), writes=[vaug.tok])
        vb5 = vb.ap.rearrange("p n (a b d) -> p n a b d", a=4, b=2)
        va5 = vaug.ap.rearrange("p n (a b) k -> p n a b k", a=4)
        for b in range(NB):
            cs = slice(b * S, (b + 1) * S)
            K.dma("pool", qb.ap, hFM[4352:4864, cs].rearrange("(c p) t -> p c t", p=128), writes=[qb.tok], sem="ld_q")
            K.dma("pool", kb_.ap, hFM[4864:5376, cs].rearrange("(c p) t -> p c t", p=128), writes=[kb_.tok], sem="ld_k")
            K.dma("pool", vb.ap, vTM[cs, 512:1024].rearrange("(n p) f -> p n f", p=128), writes=[vb.tok], sem="ld_v")
            for n4 in range(4):
                ns = slice(n4 * 4, n4 * 4 + 4)
                K.op("act", lambda e: e.copy(va5[:, ns, :, 0, 0:64], vb5[:, ns, :, 0, :]), reads=[vb.tok], writes=[vaug.tok])
                K.op("dve", lambda e: e.tensor_copy(va5[:, ns, :, 1, 64:128], vb5[:, ns, :, 1, :]), reads=[vb.tok], writes=[vaug.tok])
            units = [(h, j) for j in range(16) for h in range(8)]

            def front(it, h, j):
                hr = slice((h % 2) * 64, (h % 2) * 64 + 64)
                hc = h // 2
                nkb = min(j, 4) + 1
                i0_ = 5 - nkb
                ps_s, s_, p_ = psS[it % 2], sT[it % 2], pTt[it % 2]
                for i in range(nkb):
                    kbi = j - nkb + 1 + i
                    K.op("pe", lambda e: e.matmul(ps_s.ap[:, i * 128:(i + 1) * 128], kb_.ap[hr, hc, kbi * 128:(kbi + 1) * 128],
                                                  qb.ap[hr, hc, j * 128:(j + 1) * 128], start=True, stop=True),
                         reads=[qb.tok, kb_.tok], writes=[ps_s.tok], signal=(i == nkb - 1))
                w = nkb * 128
                K.op("dve", lambda e: e.scalar_tensor_tensor(
                    s_.ap[:, 0:w], ps_s.ap[:, 0:w], 0.125, bias.ap[:, h, i0_:5, :].rearrange("p v k -> p (v k)"),
                    ALU.mult, ALU.add), reads=[ps_s.tok, bias.tok], writes=[s_.tok])
                K.op("act", lambda e: e.activation(out=p_.ap[:, 0:w], in_=s_.ap[:, 0:w], func=AF.Exp),
                     reads=[s_.tok], writes=[p_.tok])

            def back(it, h, j):
                hr = slice((h % 2) * 64, (h % 2) * 64 + 64)
                ho = slice(64 - (h % 2) * 64, 128 - (h % 2) * 64)
                hc = h // 2
                nkb = min(j, 4) + 1
                ps_o, p_, r_ = psO[it % 2], pTt[it % 2], rd[it % 2]
                for i in range(nkb):
                    kbi = j - nkb + 1 + i
                    K.op("pe", lambda e: e.matmul(ps_o.ap[:, 0:128], vaug.ap[:, kbi, h, :], p_.ap[:, i * 128:(i + 1) * 128],
                                                  start=(i == 0), stop=(i == nkb - 1)),
                         reads=[vaug.tok, p_.tok], writes=[ps_o.tok], signal=(i == nkb - 1))
                K.op("dve", lambda e: e.reciprocal(r_.ap[hr, :], ps_o.ap[ho, 0:128]), reads=[ps_o.tok], writes=[r_.tok])
                K.op("dve", lambda e: e.tensor_tensor(of.ap[hr, hc, j * 128:(j + 1) * 128], ps_o.ap[hr, 0:128],
                                                      r_.ap[hr, :], ALU.mult),
                     reads=[ps_o.tok, r_.tok], writes=[of.tok])

            for it, (h, j) in enumerate(units):
                front(it, h, j)
                if it > 0:
                    back(it - 1, *units[it - 1])
            back(len(units) - 1, *units[-1])
            K.dma("sp", yall[1536:2048, cs].rearrange("(c p) t -> p c t", p=128), of.ap, reads=[of.tok], sem="st0")
        K.barrier()

    def layernorm_fm(z, zsq, tmp, l, gname, bname, TBW, dst, t0, stg, sti):
        yield
        K.op("act", lambda e: e.activation(out=zsq.ap, in_=z.ap, func=AF.Square), reads=[z.tok], writes=[zsq.tok])
        yield
        yield
        bM = bank()
        bQ = bank()
        for oc in range(8):
            K.op("pe", lambda e, oc=oc: e.matmul(bM.ap[:, 0:TBW], onesf.ap, z.ap[:, oc, :], start=(oc == 0), stop=(oc == 7)),
                 reads=[z.tok], writes=[bM.tok], signal=(oc == 7))
        for oc in range(8):
            K.op("pe", lambda e, oc=oc: e.matmul(bQ.ap[:, 0:TBW], onesf.ap, zsq.ap[:, oc, :], start=(oc == 0), stop=(oc == 7)),
                 reads=[zsq.tok], writes=[bQ.tok], signal=(oc == 7))
        yield
        mean = Tile(tmp.ap[:, 0, :], tmp.tok)
        msq = Tile(tmp.ap[:, 1, :], tmp.tok)
        rstd = Tile(tmp.ap[:, 2, :], tmp.tok)
        K.op("dve", lambda e: e.tensor_scalar(mean.ap, bM.ap[:, 0:TBW], 1.0 / DM, None, ALU.mult), reads=[bM.tok], writes=[tmp.tok])
        K.op("dve", lambda e: e.tensor_tensor(msq.ap, mean.ap, mean.ap, ALU.mult), reads=[tmp.tok], writes=[tmp.tok])
        K.op("dve", lambda e: e.scalar_tensor_tensor(msq.ap, bQ.ap[:, 0:TBW], 1.0 / DM, msq.ap, ALU.mult, ALU.subtract),
             reads=[bQ.tok, tmp.tok], writes=[tmp.tok])
        K.op("act", lambda e: e.activation(out=msq.ap, in_=msq.ap, func=AF.Sqrt, bias=LN_EPS), reads=[tmp.tok], writes=[tmp.tok])
        K.op("dve", lambda e: e.reciprocal(rstd.ap, msq.ap), reads=[tmp.tok], writes=[tmp.tok])
        yield
        for oc in range(8):
            K.op("dve", lambda e, oc=oc: e.tensor_tensor(zsq.ap[:, oc, :], z.ap[:, oc, :], mean.ap, ALU.subtract),
                 reads=[z.tok, tmp.tok], writes=[zsq.tok])
            K.op("dve", lambda e, oc=oc: e.tensor_tensor(zsq.ap[:, oc, :], zsq.ap[:, oc, :], rstd.ap, ALU.mult),
                 reads=[zsq.tok, tmp.tok], writes=[zsq.tok])
            st = stg[sti[0] % len(stg)]
            K.op("act", lambda e, oc=oc, st=st: e.activation(out=st.ap[:, 0:TBW], in_=zsq.ap[:, oc, :], func=AF.Identity,
                                                            scale=pcol(l, gname, oc), bias=pcol(l, bname, oc)),
                 reads=[zsq.tok], writes=[st.tok])
            K.dma("sp", dst[oc * 128:(oc + 1) * 128, t0:t0 + TBW], st.ap[:, 0:TBW], reads=[st.tok],
                  sem="st%d" % (sti[0] % len(stg)))
            sti[0] += 1
            yield

    def stage_merge(l, xsrc):
        A.reset()
        wg = A.alloc([8, 4096], BF16)
        wb = A.alloc([16, DM], BF16)
        wo = A.alloc([8, DM], BF16)
        for kc in range(8):
            K.dma("pool", wg.ap[:, kc, :], w_gate[l, kc * 128:(kc + 1) * 128, :], writes=[wg.tok], sem="ld_w")
            K.dma("pool", wo.ap[:, kc, :], w_out[l, kc * 128:(kc + 1) * 128, :], writes=[wo.tok], sem="ld_w")
        for n in range(4):
            for kc in range(4):
                K.dma("pool", wb.ap[:, n * 4 + kc, :], w_branch[l, n, kc * 128:(kc + 1) * 128, :], writes=[wb.tok], sem="ld_w")
        TBW = 256
        xb = [A.alloc([8, TBW], BF16) for _ in range(2)]
        xf = [A.alloc([8, TBW], F32) for _ in range(1)] * 2
        yb = [A.alloc([16, TBW], BF16) for _ in range(2)]
        gs = [A.alloc([TBW], F32) for _ in range(2)]
        tm = [A.alloc([TBW], F32) for _ in range(2)]
        macc = A.alloc([TBW], F32)
        mb = A.alloc([8, TBW], BF16)
        z = A.alloc([8, TBW], F32)
        zsq = A.alloc([8, TBW], F32)
        tmp = A.alloc([3, TBW], F32)
        stg = [A.alloc([TBW], F32) for _ in range(3)]
        sti = [0]
        xs3 = xsrc.rearrange("(kc p) t -> p kc t", p=128)
        ya3 = yall.rearrange("(kc p) t -> p kc t", p=128)
        gic = [0]

        def partA(tb):
            ts = slice(tb * TBW, (tb + 1) * TBW)
            x_b, y_b = xb[tb % 2], yb[tb % 2]
            K.dma("pool", x_b.ap, xs3[:, :, ts], writes=[x_b.tok], sem="ld_x%d" % (tb % 2))
            K.dma("sp", y_b.ap, ya3[:, :, ts], writes=[y_b.tok], sem="ld_y%d" % (tb % 2))
            for oc in range(8):
                for n in range(4):
                    gi = gic[0]
                    bG = bank()
                    bB = bank()
                    for kc in range(8):
                        K.op("pe", lambda e, kc=kc: e.matmul(bG.ap[:, 0:TBW], wg.ap[:, kc, n * 1024 + oc * 128:n * 1024 + (oc + 1) * 128],
                                                             x_b.ap[:, kc, :], start=(kc == 0), stop=(kc == 7)),
                             reads=[wg.tok, x_b.tok], writes=[bG.tok], signal=(kc == 7))
                    for kc in range(4):
                        K.op("pe", lambda e, kc=kc: e.matmul(bB.ap[:, 0:TBW], wb.ap[:, n * 4 + kc, oc * 128:(oc + 1) * 128],
                                                             y_b.ap[:, n * 4 + kc, :], start=(kc == 0), stop=(kc == 3)),
                             reads=[wb.tok, y_b.tok], writes=[bB.tok], signal=(kc == 3))
                    g_ = gs[gi % 2]
                    t_ = tm[gi % 2]
                    gic[0] += 1
                    K.op("act", lambda e: e.activation(out=g_.ap, in_=bG.ap[:, 0:TBW], func=AF.Sigmoid, bias=pcol(l, "bg", n * 8 + oc)),
                         reads=[bG.tok], writes=[g_.tok])
                    if n == 0:
                        K.op("dve", lambda e: e.tensor_tensor(macc.ap, g_.ap, bB.ap[:, 0:TBW], ALU.mult), reads=[g_.tok, bB.tok],
                             writes=[macc.tok])
                    else:
                        K.op("dve", lambda e: e.tensor_tensor(t_.ap, g_.ap, bB.ap[:, 0:TBW], ALU.mult), reads=[g_.tok, bB.tok],
                             writes=[t_.tok])
                        dst = mb.ap[:, oc, :] if n == 3 else macc.ap
                        dtok = mb.tok if n == 3 else macc.tok
                        K.op("dve", lambda e, dst=dst: e.tensor_tensor(dst, macc.ap, t_.ap, ALU.add), reads=[macc.tok, t_.tok],
                             writes=[dtok])
                    yield

        def partB(tb):
            ts = slice(tb * TBW, (tb + 1) * TBW)
            x_f = xf[tb % 2]
            K.dma("sp", x_f.ap, xs3[:, :, ts], writes=[x_f.tok], sem="ld_xf0")
            for oc in range(8):
                bZ = bank()
                for kc in range(8):
                    K.op("pe", lambda e, kc=kc: e.matmul(bZ.ap[:, 0:TBW], wo.ap[:, kc, oc * 128:(oc + 1) * 128], mb.ap[:, kc, :],
                                                         start=(kc == 0), stop=(kc == 7)),
                         reads=[wo.tok, mb.tok], writes=[bZ.tok], signal=(kc == 7))
                K.op("dve", lambda e: e.scalar_tensor_tensor(z.ap[:, oc, :], x_f.ap[:, oc, :], ALPHA, bZ.ap[:, 0:TBW], ALU.mult, ALU.add),
                     reads=[x_f.tok, bZ.tok], writes=[z.tok])

        ntb = T // TBW
        interleave(partA(0))
        for tb in range(ntb):
            partB(tb)
            interleave(layernorm_fm(z, zsq, tmp, l, "l1g", "l1b", TBW, x1d, tb * TBW, stg, sti),
                       partA(tb + 1) if tb + 1 < ntb else None)
        K.barrier()

    def stage_ple(l):
        A.reset()
        TBW = 512
        wp = A.alloc([2, DM], BF16)
        wq = A.alloc([8, DM], BF16)
        for kc in range(8):
            K.dma("pool", wq.ap[:, kc, :], w_pg[l, kc * 128:(kc + 1) * 128, :], writes=[wq.tok], sem="ld_w")
        for kc in range(2):
            K.dma("pool", wp.ap[:, kc, :], w_ple[l, kc * 128:(kc + 1) * 128, :], writes=[wp.tok], sem="ld_w")
        xb = [A.alloc([8, TBW], BF16) for _ in range(2)]
        xf = [A.alloc([8, TBW], F32) for _ in range(2)]
        pb = [A.alloc([2, TBW], BF16) for _ in range(2)]
        sg = [A.alloc([TBW], F32) for _ in range(2)]
        stg = [A.alloc([TBW], F32) for _ in range(3)]
        si = 0
        xs3 = x1d.rearrange("(kc p) t -> p kc t", p=128)
        ps3 = pT[l].rearrange("(kc p) t -> p kc t", p=128)
        for tb in range(T // TBW):
            ts = slice(tb * TBW, (tb + 1) * TBW)
            x_b, x_f, p_b = xb[tb % 2], xf[tb % 2], pb[tb % 2]
            K.dma("pool", x_b.ap, xs3[:, :, ts], writes=[x_b.tok], sem="ld_x%d" % (tb % 2))
            K.dma("sp", x_f.ap, xs3[:, :, ts], writes=[x_f.tok], sem="ld_xf%d" % (tb % 2))
            K.dma("pool", p_b.ap, ps3[:, :, ts], writes=[p_b.tok], sem="ld_y%d" % (tb % 2))
            for oc in range(8):
                bP = bank()
                bQ = bank()
                osl = slice(oc * 128, (oc + 1) * 128)
                for kc in range(2):
                    K.op("pe", lambda e: e.matmul(bP.ap, wp.ap[:, kc, osl], p_b.ap[:, kc, :], start=(kc == 0), stop=(kc == 1)),
                         reads=[wp.tok, p_b.tok], writes=[bP.tok], signal=(kc == 1))
                for kc in range(8):
                    K.op("pe", lambda e: e.matmul(bQ.ap, wq.ap[:, kc, osl], x_b.ap[:, kc, :], start=(kc == 0), stop=(kc == 7)),
                         reads=[wq.tok, x_b.tok], writes=[bQ.tok], signal=(kc == 7))
                s_ = sg[oc % 2]
                st = stg[si % 3]
                K.op("act", lambda e: e.activation(out=s_.ap, in_=bQ.ap, func=AF.Sigmoid, bias=pcol(l, "bpg", oc)),
                     reads=[bQ.tok], writes=[s_.tok])
                K.op("dve", lambda e: e.tensor_tensor(s_.ap, s_.ap, bP.ap, ALU.mult), reads=[s_.tok, bP.tok], writes=[s_.tok])
                K.op("dve", lambda e: e.scalar_tensor_tensor(st.ap, x_f.ap[:, oc, :], ALPHA, s_.ap, ALU.mult, ALU.add),
                     reads=[x_f.tok, s_.tok], writes=[st.tok])
                K.dma("sp", based[oc * 128:(oc + 1) * 128, ts], st.ap, reads=[st.tok], sem="st%d" % (si % 3))
                si += 1
        K.barrier()

    def stage_ffn(l, dst):
        A.reset()
        TBW = 256
        w1 = A.alloc([8, 4096], BF16)
        w2 = A.alloc([32, DM], BF16)
        for kc in range(8):
            K.dma("pool", w1.ap[:, kc, :], w_ff1[l, kc * 128:(kc + 1) * 128, :], writes=[w1.tok], sem="ld_w")
        for kc in range(32):
            K.dma("pool", w2.ap[:, kc, :], w_ff2[l, kc * 128:(kc + 1) * 128, :], writes=[w2.tok], sem="ld_w")
        xb = [A.alloc([8, TBW], BF16) for _ in range(2)]
        hid = A.alloc([32, TBW], BF16)
        zsq = A.alloc([8, TBW], F32)
        rl = [A.alloc([TBW], F32) for _ in range(2)]
        z = A.alloc([8, TBW], F32)
        tmp = A.alloc([3, TBW], F32)
        stg = [A.alloc([TBW], F32) for _ in range(3)]
        sti = [0]
        xs3 = x1d.rearrange("(kc p) t -> p kc t", p=128)
        bs3 = based.rearrange("(kc p) t -> p kc t", p=128)
        ric = [0]

        def ff1(tb):
            ts = slice(tb * TBW, (tb + 1) * TBW)
            x_b = xb[tb % 2]
            K.dma("pool", x_b.ap, xs3[:, :, ts], writes=[x_b.tok], sem="ld_x%d" % (tb % 2))
            for fc in range(32):
                bH = bank()
                for kc in range(8):
                    K.op("pe", lambda e: e.matmul(bH.ap[:, 0:TBW], w1.ap[:, kc, fc * 128:(fc + 1) * 128], x_b.ap[:, kc, :],
                                                  start=(kc == 0), stop=(kc == 7)),
                         reads=[w1.tok, x_b.tok], writes=[bH.tok], signal=(kc == 7))
                r_ = rl[ric[0] % 2]
                ric[0] += 1
                K.op("act", lambda e: e.activation(out=r_.ap, in_=bH.ap[:, 0:TBW], func=AF.Relu), reads=[bH.tok], writes=[r_.tok])
                K.op("dve", lambda e: e.tensor_tensor(hid.ap[:, fc, :], r_.ap, r_.ap, ALU.mult), reads=[r_.tok], writes=[hid.tok])
                yield

        def ff2(tb):
            ts = slice(tb * TBW, (tb + 1) * TBW)
            K.dma("sp", z.ap, bs3[:, :, ts], writes=[z.tok], sem="ld_xf0")
            for oc in range(8):
                bF = bank()
                osl = slice(oc * 128, (oc + 1) * 128)
                for fc in range(32):
                    K.op("pe", lambda e: e.matmul(bF.ap[:, 0:TBW], w2.ap[:, fc, osl], hid.ap[:, fc, :],
                                                  start=(fc == 0), stop=(fc == 31)),
                         reads=[w2.tok, hid.tok], writes=[bF.tok], signal=(fc == 31))
                K.op("dve", lambda e: e.tensor_tensor(z.ap[:, oc, :], z.ap[:, oc, :], bF.ap[:, 0:TBW], ALU.add),
                     reads=[z.tok, bF.tok], writes=[z.tok])

        ntb = T // TBW
        interleave(ff1(0))
        for tb in range(ntb):
            ff2(tb)
            interleave(layernorm_fm(z, zsq, tmp, l, "l2g", "l2b", TBW, dst, tb * TBW, stg, sti),
                       ff1(tb + 1) if tb + 1 < ntb else None)
        K.barrier()

    def pair():
        i = bank_i[0]
        if i % 2:
            i += 1
        bank_i[0] = i + 2
        a, b_ = banks[i % 8], banks[(i + 1) % 8]
        o = (i % 8) * 512
        return psum_t[:, o:o + 1024], [a.tok, b_.tok]

    def stage_rwkv(l):
        A.reset()
        RW = 256
        NCH = RW // 64
        w2s = A.alloc([512], F32)
        a2s = A.alloc([512], F32)
        g2s = A.alloc([512], F32)
        muvb = A.alloc([512], F32)
        wtok = Tok()
        K.dma("sp", w2s.ap[0:64, :], rw2[l], writes=[wtok], sem="ld_w")
        K.dma("sp", a2s.ap[64:128, :], ra2[l], writes=[wtok], sem="ld_w")
        K.dma("sp", g2s.ap, rg2[l], writes=[wtok], sem="ld_w")
        K.dma("sp", muvb.ap[0:64, :], muv[l], writes=[wtok], sem="ld_w")
        f = lambda: A.alloc([RW], F32)
        zwa, zg = A.alloc([RW + 1], F32), A.alloc([RW + 1], F32)
        zr, zk, zv = A.alloc([RW + 1], F32), A.alloc([RW + 1], F32), A.alloc([RW + 1], F32)
        zwap, zgp, dd = f(), f(), f()
        rp, kp, vp = f(), f(), f()
        sgw, av, kk, kkn, k2, bv, cum, cumx, e1, e2, e3, t1_, t2_ = [f() for _ in range(13)]
        vraw = A.alloc([NCH, 512], F32)
        vprev = A.alloc([NCH, 512], F32)

        class BSet:
            pass

        sets = []
        for _ in range(2):
            s_ = BSet()
            s_.gall = A.alloc([4, RW], F32)
            s_.bonus = A.alloc([4, RW], F32)
            s_.AR = A.alloc([8, NCH, 2, 64], BF16)
            s_.KBt = A.alloc([8, NCH, 2, 64], BF16)
            s_.gam = A.alloc([8, NCH], F32)
            s_.VT = A.alloc([NCH, 512], BF16)
            s_.KT = A.alloc([NCH, 512], BF16)
            s_.BT = A.alloc([NCH, 512], BF16)
            s_.AKR = A.alloc([NCH, 8, 128], BF16)
            s_.ABR = A.alloc([NCH, 8, 64], BF16)
            s_.Tbf = A.alloc([NCH, 8, 64], BF16)
            sets.append(s_)
        Ma, Mb, MTa, MTb, Tb16 = [A.alloc([16, 64], BF16) for _ in range(5)]
        T32 = A.alloc([16, 64], F32)
        ctoks = {id(t_): [Tok(), Tok()] for t_ in (Ma, Mb, MTa, MTb, Tb16, T32)}
        WT = A.alloc([512], BF16)
        UT = A.alloc([512], BF16)
        Sf = A.alloc([8, 64], F32)
        Sb = A.alloc([8, 64], BF16)
        tmpS = A.alloc([8, 64], F32)
        Y = A.alloc([NCH, 512], F32)
        Ysq = A.alloc([NCH, 512], F32)
        st1 = A.alloc([NCH * 8], F32)
        st2 = A.alloc([NCH * 8], F32)
        st3 = A.alloc([NCH * 8], F32)
        to1 = A.alloc([RW], F32)
        ostg = [A.alloc([RW], BF16) for _ in range(2)]
        osi = [0]
        P64 = slice(0, 64)
        PS = [slice(0, 64), slice(64, 128)]
        m1b = mask1.ap[0:64, :].unsqueeze(1).to_broadcast([64, 8, 128])
        mSb = mask1.ap[0:64, 0:64].unsqueeze(1).to_broadcast([64, 8, 64])
        mIb = mask1.ap[0:64, 64:128].unsqueeze(1).to_broadcast([64, 8, 64])
        mLb = masksl.ap[0:64, :].unsqueeze(1).to_broadcast([64, 8, 64])
        idb_ = ident.ap[0:64, 0:64].unsqueeze(1).to_broadcast([64, 8, 64])

        def lerp(zt, out, mucol, parts=slice(0, 128)):
            K.op("dve", lambda e: e.tensor_tensor(dd.ap[parts, :], zt.ap[parts, 0:RW], zt.ap[parts, 1:RW + 1], ALU.subtract),
                 reads=[zt.tok], writes=[dd.tok])
            K.op("dve", lambda e: e.scalar_tensor_tensor(out.ap[parts, :], dd.ap[parts, :], mucol, zt.ap[parts, 1:RW + 1],
                                                         ALU.mult, ALU.add), reads=[dd.tok, zt.tok], writes=[out.tok])

        def loadz(zt, row0, b, blk, nm):
            t0 = b * S + blk * RW
            if blk == 0:
                K.op("dve", lambda e: e.memset(zt.ap[:, 0:1], 0.0), writes=[zt.tok])
                K.dma("sp", zt.ap[:, 1:RW + 1], hFM[row0:row0 + 128, t0:t0 + RW], writes=[zt.tok], sem="ld_" + nm)
            else:
                K.dma("sp", zt.ap, hFM[row0:row0 + 128, t0 - 1:t0 + RW], writes=[zt.tok], sem="ld_" + nm)

        def genA(b, blk, st):
            AR, KBt, gam, gall, bonus, VT, KT, BT, AKR, ABR, Tbf = (st.AR, st.KBt, st.gam, st.gall, st.bonus, st.VT, st.KT,
                                                                    st.BT, st.AKR, st.ABR, st.Tbf)
            t0 = b * S + blk * RW
            loadz(zwa, 4096, b, blk, "zwa")
            loadz(zg, 4224, b, blk, "zg")
            lerp(zwa, zwap, pcol(l, "mu", 12))
            lerp(zg, zgp, pcol(l, "mu", 13))
            K.op("act", lambda e: e.activation(out=zwap.ap[0:64, :], in_=zwap.ap[0:64, :], func=AF.Tanh),
                 reads=[zwap.tok], writes=[zwap.tok])
            K.op("act", lambda e: e.activation(out=zgp.ap, in_=zgp.ap, func=AF.Sigmoid), reads=[zgp.tok], writes=[zgp.tok])
            yield
            for hp in range(4):
                hs = slice(hp * 128, (hp + 1) * 128)
                loadz(zr, 2560 + hp * 128, b, blk, "zr")
                loadz(zk, 3072 + hp * 128, b, blk, "zk")
                loadz(zv, 3584 + hp * 128, b, blk, "zv")
                lerp(zr, rp, pcol(l, "mu", hp))
                lerp(zk, kp, pcol(l, "mu", 4 + hp))
                lerp(zv, vp, pcol(l, "mu", 8 + hp))
                bW, bA, bG = bank(), bank(), bank()
                K.op("pe", lambda e: e.matmul(bW.ap[:, 0:RW], w2s.ap[0:64, hs], zwap.ap[0:64, :], start=True, stop=True),
                     reads=[wtok, zwap.tok], writes=[bW.tok])
                K.op("pe", lambda e: e.matmul(bA.ap[:, 0:RW], a2s.ap[64:128, hs], zwap.ap[64:128, :], start=True, stop=True),
                     reads=[wtok, zwap.tok], writes=[bA.tok])
                K.op("pe", lambda e: e.matmul(bG.ap[:, 0:RW], g2s.ap[:, hs], zgp.ap, start=True, stop=True),
                     reads=[wtok, zgp.tok], writes=[bG.tok])
                yield
                K.op("act", lambda e: e.activation(out=sgw.ap, in_=bW.ap[:, 0:RW], func=AF.Sigmoid, bias=pcol(l, "w0", hp)),
                     reads=[bW.tok], writes=[sgw.tok])
                K.op("act", lambda e: e.activation(out=av.ap, in_=bA.ap[:, 0:RW], func=AF.Sigmoid, bias=pcol(l, "a0", hp)),
                     reads=[bA.tok], writes=[av.tok])
                K.op("act", lambda e: e.copy(gall.ap[:, hp, :], bG.ap[:, 0:RW]), reads=[bG.tok], writes=[gall.tok])
                K.op("dve", lambda e: e.tensor_scalar(kk.ap, kp.ap, pcol(l, "kk", hp), None, ALU.mult), reads=[kp.tok],
                     writes=[kk.tok])
                K.op("act", lambda e: e.activation(out=t1_.ap, in_=kk.ap, func=AF.Square), reads=[kk.tok], writes=[t1_.tok])
                bN = bank()
                K.op("pe", lambda e: e.matmul(bN.ap[:, 0:RW], onesbd.ap, t1_.ap, start=True, stop=True), reads=[t1_.tok],
                     writes=[bN.tok])
                yield
                K.op("act", lambda e: e.activation(out=t2_.ap, in_=bN.ap[:, 0:RW], func=AF.Sqrt), reads=[bN.tok],
                     writes=[t2_.tok])
                K.op("dve", lambda e: e.tensor_scalar(t2_.ap, t2_.ap, 1e-12, None, ALU.max), reads=[t2_.tok], writes=[t2_.tok])
                K.op("dve", lambda e: e.reciprocal(t1_.ap, t2_.ap), reads=[t2_.tok, t1_.tok], writes=[t1_.tok])
                K.op("dve", lambda e: e.tensor_tensor(kkn.ap, kk.ap, t1_.ap, ALU.mult), reads=[kk.tok, t1_.tok], writes=[kkn.tok])
                K.op("dve", lambda e: e.tensor_scalar(t2_.ap, av.ap, pcol(l, "ka", hp), pcol(l, "omka", hp), ALU.mult, ALU.add),
                     reads=[av.tok, t2_.tok], writes=[t2_.tok])
                K.op("dve", lambda e: e.tensor_tensor(k2.ap, kp.ap, t2_.ap, ALU.mult), reads=[kp.tok, t2_.tok], writes=[k2.tok])
                K.op("dve", lambda e: e.tensor_tensor(bv.ap, kkn.ap, av.ap, ALU.mult), reads=[kkn.tok, av.tok], writes=[bv.tok])
                K.op("dve", lambda e: e.scalar_tensor_tensor(t1_.ap, rp.ap, pcol(l, "rk", hp), k2.ap, ALU.mult, ALU.mult),
                     reads=[rp.tok, k2.tok, t1_.tok], writes=[t1_.tok])
                bR = bank()
                K.op("pe", lambda e: e.matmul(bR.ap[:, 0:RW], onesbd.ap, t1_.ap, start=True, stop=True), reads=[t1_.tok],
                     writes=[bR.tok])
                yield
                K.op("dve", lambda e: e.tensor_tensor(bonus.ap[:, hp, :], bR.ap[:, 0:RW], vp.ap, ALU.mult),
                     reads=[bR.tok, vp.tok], writes=[bonus.tok])
                K.op("dve", lambda e: e.tensor_tensor_scan(cum.ap, rmask.ap[:, 0:RW], sgw.ap, 0.0, ALU.mult, ALU.add),
                     reads=[sgw.tok], writes=[cum.tok])
                K.op("dve", lambda e: e.tensor_tensor(cumx.ap, cum.ap, sgw.ap, ALU.subtract), reads=[cum.tok, sgw.tok],
                     writes=[cumx.tok])
                K.op("act", lambda e: e.activation(out=e1.ap, in_=cumx.ap, func=AF.Exp, scale=-C0), reads=[cumx.tok], writes=[e1.tok])
                K.op("act", lambda e: e.activation(out=e2.ap, in_=cum.ap, func=AF.Exp, scale=-C0), reads=[cum.tok], writes=[e2.tok])
                K.op("act", lambda e: e.activation(out=e3.ap, in_=cum.ap, func=AF.Exp, scale=C0), reads=[cum.tok], writes=[e3.tok])
                yield
                for hh in range(2):
                    h = 2 * hp + hh
                    pr = slice(hh * 64, hh * 64 + 64)
                    v3 = lambda t_: t_.ap[pr, :].rearrange("p (c j) -> p c j", c=NCH)
                    K.op("dve", lambda e: e.scalar_tensor_tensor(AR.ap[P64, h, :, 0, :], v3(kkn), -1.0, v3(e1), ALU.mult, ALU.mult),
                         reads=[kkn.tok, e1.tok], writes=[AR.tok])
                    K.op("pool" if hh == 0 else "dve", lambda e: e.tensor_tensor(AR.ap[P64, h, :, 1, :], v3(rp), v3(e2), ALU.mult),
                         reads=[rp.tok, e2.tok], writes=[AR.tok])
                    K.op("dve", lambda e: e.tensor_tensor(KBt.ap[P64, h, :, 0, :], v3(k2), v3(e3), ALU.mult),
                         reads=[k2.tok, e3.tok], writes=[KBt.tok])
                    K.op("pool" if hh == 0 else "dve", lambda e: e.tensor_tensor(KBt.ap[P64, h, :, 1, :], v3(bv), v3(e3), ALU.mult),
                         reads=[bv.tok, e3.tok], writes=[KBt.tok])
                    K.op("act", lambda e: e.activation(out=gam.ap[P64, h, :], in_=v3(cum)[:, :, 63], func=AF.Exp, scale=-C0),
                         reads=[cum.tok], writes=[gam.tok])
                yield
            v3d = vTM[t0:t0 + RW, 0:512].rearrange("(c p) f -> p c f", p=64)
            K.dma("sp", vraw.ap[P64], v3d, writes=[vraw.tok], sem="ld_vr")
            if blk == 0:
                K.op("dve", lambda e: e.memset(vprev.ap[0:1, 0, :], 0.0), writes=[vprev.tok])
                K.dma("sp", vprev.ap[1:64, 0, :], vTM[t0:t0 + 63, 0:512], writes=[vprev.tok], sem="ld_vp")
                K.dma("sp", vprev.ap[P64, 1:NCH, :],
                      vTM[t0 + 63:t0 + RW - 1, 0:512].rearrange("(c p) f -> p c f", p=64), writes=[vprev.tok], sem="ld_vp")
            else:
                K.dma("sp", vprev.ap[P64], vTM[t0 - 1:t0 + RW - 1, 0:512].rearrange("(c p) f -> p c f", p=64),
                      writes=[vprev.tok], sem="ld_vp")
            K.op("pool", lambda e: e.tensor_tensor(vprev.ap[P64], vprev.ap[P64], vraw.ap[P64], ALU.subtract),
                 reads=[vraw.tok, vprev.tok], writes=[vprev.tok])
            K.op("pool", lambda e: e.tensor_tensor(vprev.ap[P64], vprev.ap[P64],
                                                   muvb.ap[P64, :].unsqueeze(1).to_broadcast([64, NCH, 512]), ALU.mult),
                 reads=[vprev.tok, wtok], writes=[vprev.tok])
            K.op("pool", lambda e: e.tensor_tensor(VT.ap[P64], vprev.ap[P64], vraw.ap[P64], ALU.add),
                 reads=[vraw.tok, vprev.tok], writes=[VT.tok])
            yield
            for which, dstT in ((0, KT), (1, BT)):
                for cg in range(NCH // 2):
                    bk = bank()
                    pb = bk.ap.bitcast(BF16)
                    for ci in range(2):
                        c = cg * 2 + ci
                        for h in range(8):
                            K.op("pe", lambda e: e.transpose(pb[P64, ci * 512 + h * 64:ci * 512 + (h + 1) * 64],
                                                             KBt.ap[P64, h, c, which, :], identb.ap[0:64, 0:64]),
                                 reads=[KBt.tok], writes=[bk.tok], signal=(ci == 1 and h == 7))
                    K.op(evac_eng(), copy_op("act" if evi[0] % 2 else "dve",
                                             dstT.ap[P64, cg * 2:cg * 2 + 2, :].rearrange("p c f -> p (c f)"), pb[P64, :]),
                         reads=[bk.tok], writes=[dstT.tok])
                    yield
            mSb2 = mask1.ap[64:128, 0:64].unsqueeze(1).to_broadcast([64, 8, 64])
            mIb2 = mask1.ap[64:128, 64:128].unsqueeze(1).to_broadcast([64, 8, 64])
            for c in range(NCH):
                s_i = c // 2
                sl0 = (c % 2) * 8
                psl = PS[s_i]
                pm, pmt = pair()
                b3 = bank()
                for h in range(8):
                    arr = AR.ap[P64, h, c, :, :].rearrange("p a j -> p (a j)")
                    kb2 = KBt.ap[P64, h, c, :, :].rearrange("p a j -> p (a j)")
                    K.op("pe", lambda e: e.matmul(pm[:, h * 128:(h + 1) * 128], kb2, arr, start=True, stop=True),
                         reads=[KBt.tok, AR.tok], writes=pmt, signal=(h == 7))
                for h in range(8):
                    K.op("pe", lambda e: e.matmul(b3.ap[P64, h * 64:(h + 1) * 64], AR.ap[P64, h, c, 0, :], KBt.ap[P64, h, c, 1, :],
                                                  start=True, stop=True),
                         reads=[KBt.tok, AR.tok], writes=[b3.tok], signal=(h == 7))
                yield
                p1v = pm[P64, :].rearrange("p (h k) -> p h k", h=8)
                p2v = pm[64:128, :].rearrange("p (h k) -> p h k", h=8)
                b3v = b3.ap[P64, :].rearrange("p (h k) -> p h k", h=8)
                K.op("dve", lambda e: e.tensor_tensor(AKR.ap[P64, c], p1v, m1b, ALU.mult), reads=pmt, writes=[AKR.tok])
                K.op("dve", lambda e: e.tensor_tensor(Ma.ap[psl, sl0:sl0 + 8, :], p2v[:, :, 0:64], mSb2, ALU.mult), reads=pmt,
                     writes=[ctoks[id(Ma)][s_i]])
                K.op("dve", lambda e: e.tensor_tensor(ABR.ap[P64, c], p2v[:, :, 64:128], mIb2, ALU.mult), reads=pmt,
                     writes=[ABR.tok])
                K.op("dve", lambda e: e.tensor_tensor(MTa.ap[psl, sl0:sl0 + 8, :], b3v, mLb, ALU.mult), reads=[b3.tok],
                     writes=[ctoks[id(MTa)][s_i]])
                idb_s = ident.ap[psl, 64 * s_i:64 * s_i + 64].unsqueeze(1).to_broadcast([64, 8, 64])
                K.op("dve", lambda e: e.tensor_tensor(T32.ap[psl, sl0:sl0 + 8, :], Ma.ap[psl, sl0:sl0 + 8, :], idb_s, ALU.add),
                     reads=[ctoks[id(Ma)][s_i]], writes=[ctoks[id(T32)][s_i]])
                K.op("act", lambda e: e.copy(Tb16.ap[psl, sl0:sl0 + 8, :], T32.ap[psl, sl0:sl0 + 8, :]),
                     reads=[ctoks[id(T32)][s_i]], writes=[ctoks[id(Tb16)][s_i]])
                yield
            cur = [[Ma, MTa], [Ma, MTa]]
            nxt = [[Mb, MTb], [Mb, MTb]]
            tk = lambda t_, s_i: ctoks[id(t_)][s_i]
            for i in range(1, 6):
                bMT = [pair(), pair()]
                bM = [pair(), pair()] if i < 5 else None
                if i < 5:
                    for q in range(16):
                        for s_i in range(2):
                            psl = PS[s_i]
                            M, MT = cur[s_i]
                            K.op("pe", lambda e: e.matmul(bM[s_i][0][psl, q * 64:(q + 1) * 64], MT.ap[psl, q, :], M.ap[psl, q, :],
                                                          start=True, stop=True),
                                 reads=[tk(M, s_i), tk(MT, s_i)], writes=bM[s_i][1], signal=(q == 15))
                for q in range(16):
                    for s_i in range(2):
                        psl = PS[s_i]
                        M, MT = cur[s_i]
                        K.op("pe", lambda e: e.matmul(bMT[s_i][0][psl, q * 64:(q + 1) * 64], M.ap[psl, q, :], MT.ap[psl, q, :],
                                                      start=True, stop=True),
                             reads=[tk(M, s_i), tk(MT, s_i)], writes=bMT[s_i][1], signal=(q == 15))
                for s_i in range(2):
                    psl = PS[s_i]
                    Mn, MTn = nxt[s_i]
                    if i < 5:
                        K.op("act", lambda e: e.copy(Mn.ap[psl].rearrange("p h k -> p (h k)"), bM[s_i][0][psl, :]),
                             reads=bM[s_i][1], writes=[tk(Mn, s_i)])
                    K.op("dve", lambda e: e.tensor_copy(MTn.ap[psl].rearrange("p h k -> p (h k)"), bMT[s_i][0][psl, :]),
                         reads=bMT[s_i][1], writes=[tk(MTn, s_i)])
                yield
                bT = [pair(), pair()]
                for q in range(16):
                    for s_i in range(2):
                        psl = PS[s_i]
                        MTn = nxt[s_i][1]
                        K.op("pe", lambda e: e.matmul(bT[s_i][0][psl, q * 64:(q + 1) * 64], MTn.ap[psl, q, :], Tb16.ap[psl, q, :],
                                                      start=True, stop=True),
                             reads=[tk(MTn, s_i), tk(Tb16, s_i)], writes=bT[s_i][1], signal=(q == 15))
                for s_i in range(2):
                    psl = PS[s_i]
                    t32v = T32.ap[psl].rearrange("p h k -> p (h k)")
                    if i < 5:
                        K.op("dve", lambda e: e.tensor_tensor(t32v, t32v, bT[s_i][0][psl, :], ALU.add),
                             reads=bT[s_i][1] + [tk(T32, s_i)], writes=[tk(T32, s_i)])
                        K.op("act", lambda e: e.copy(Tb16.ap[psl].rearrange("p h k -> p (h k)"), t32v),
                             reads=[tk(T32, s_i)], writes=[tk(Tb16, s_i)])
                    else:
                        K.op("dve", lambda e: e.tensor_tensor(Tbf.ap[P64, 2 * s_i:2 * s_i + 2].rearrange("p c h k -> p (c h k)"),
                                                              t32v, bT[s_i][0][psl, :], ALU.add),
                             reads=bT[s_i][1] + [tk(T32, s_i)], writes=[Tbf.tok])
                    cur[s_i], nxt[s_i] = nxt[s_i], cur[s_i]
                yield

        def genB(b, blk, st):
            AR, KBt, gam, gall, bonus, VT, KT, BT, AKR, ABR, Tbf = (st.AR, st.KBt, st.gam, st.gall, st.bonus, st.VT, st.KT,
                                                                    st.BT, st.AKR, st.ABR, st.Tbf)
            t0 = b * S + blk * RW
            if blk == 0:
                K.op("dve", lambda e: e.memset(Sf.ap[P64], 0.0), writes=[Sf.tok])
                K.op("dve", lambda e: e.memset(Sb.ap[P64], 0.0), writes=[Sb.tok])
            for c in range(NCH):
                bW = bank()
                for h in range(8):
                    hsl = slice(h * 64, (h + 1) * 64)
                    K.op("pe", lambda e: e.matmul(bW.ap[P64, hsl], AR.ap[P64, h, c, 0, :], Sb.ap[P64, h, :], start=True, stop=False),
                         reads=[AR.tok, Sb.tok], writes=[bW.tok], signal=False)
                    K.op("pe", lambda e: e.matmul(bW.ap[P64, hsl], AKR.ap[P64, c, h, 0:64], VT.ap[P64, c, hsl], start=False, stop=True),
                         reads=[AKR.tok, VT.tok], writes=[bW.tok], signal=(h == 7))
                yield
                K.op("act", lambda e: e.copy(WT.ap[P64, :], bW.ap[P64, :]), reads=[bW.tok], writes=[WT.tok])
                bU = bank()
                for h in range(8):
                    hsl = slice(h * 64, (h + 1) * 64)
                    K.op("pe", lambda e: e.matmul(bU.ap[P64, hsl], Tbf.ap[P64, c, h, :], WT.ap[P64, hsl], start=True, stop=True),
                         reads=[Tbf.tok, WT.tok], writes=[bU.tok], signal=(h == 7))
                yield
                K.op("dve", lambda e: e.tensor_copy(UT.ap[P64, :], bU.ap[P64, :]), reads=[bU.tok], writes=[UT.tok])
                bS = bank()
                for h in range(8):
                    hsl = slice(h * 64, (h + 1) * 64)
                    K.op("pe", lambda e: e.matmul(bS.ap[P64, hsl], KT.ap[P64, c, hsl], VT.ap[P64, c, hsl], start=True, stop=False),
                         reads=[KT.tok, VT.tok], writes=[bS.tok], signal=False)
                    K.op("pe", lambda e: e.matmul(bS.ap[P64, hsl], BT.ap[P64, c, hsl], UT.ap[P64, hsl], start=False, stop=True),
                         reads=[BT.tok, UT.tok], writes=[bS.tok], signal=(h == 7))
                bY = bank()
                for h in range(8):
                    hsl = slice(h * 64, (h + 1) * 64)
                    K.op("pe", lambda e: e.matmul(bY.ap[P64, hsl], AR.ap[P64, h, c, 1, :], Sb.ap[P64, h, :], start=True, stop=False),
                         reads=[AR.tok, Sb.tok], writes=[bY.tok], signal=False)
                    K.op("pe", lambda e: e.matmul(bY.ap[P64, hsl], ABR.ap[P64, c, h, :], UT.ap[P64, hsl], start=False, stop=False),
                         reads=[ABR.tok, UT.tok], writes=[bY.tok], signal=False)
                    K.op("pe", lambda e: e.matmul(bY.ap[P64, hsl], AKR.ap[P64, c, h, 64:128], VT.ap[P64, c, hsl], start=False, stop=True),
                         reads=[AKR.tok, VT.tok], writes=[bY.tok], signal=(h == 7))
                yield
                K.op("dve", lambda e: e.tensor_tensor(tmpS.ap[P64].rearrange("p h k -> p (h k)"), bS.ap[P64, :],
                                                      Sf.ap[P64].rearrange("p h k -> p (h k)"), ALU.add),
                     reads=[bS.tok, Sf.tok], writes=[tmpS.tok])
                K.op("dve", lambda e: e.tensor_tensor(Sf.ap[P64], tmpS.ap[P64], gam.ap[P64, :, c:c + 1].to_broadcast([64, 8, 64]),
                                                      ALU.mult), reads=[tmpS.tok, gam.tok], writes=[Sf.tok])
                K.op("act", lambda e: e.copy(Sb.ap[P64], Sf.ap[P64]), reads=[Sf.tok], writes=[Sb.tok])
                K.op("act", lambda e: e.copy(Y.ap[P64, c, :], bY.ap[P64, :]), reads=[bY.tok], writes=[Y.tok])
                yield
            Yv = Y.ap[P64].rearrange("p c (h v) -> p (c h) v", h=8)
            Ysv = Ysq.ap[P64].rearrange("p c (h v) -> p (c h) v", h=8)
            K.op("dve", lambda e: e.tensor_reduce(st1.ap[P64, :], Yv, AX.X, ALU.add), reads=[Y.tok], writes=[st1.tok])
            K.op("act", lambda e: e.activation(out=Ysq.ap[P64], in_=Y.ap[P64], func=AF.Square), reads=[Y.tok], writes=[Ysq.tok])
            yield
            K.op("dve", lambda e: e.tensor_reduce(st2.ap[P64, :], Ysv, AX.X, ALU.add), reads=[Ysq.tok], writes=[st2.tok])
            K.op("dve", lambda e: e.tensor_scalar(st1.ap[P64, :], st1.ap[P64, :], 1.0 / 64, None, ALU.mult), reads=[st1.tok],
                 writes=[st1.tok])
            K.op("dve", lambda e: e.tensor_tensor(st3.ap[P64, :], st1.ap[P64, :], st1.ap[P64, :], ALU.mult), reads=[st1.tok],
                 writes=[st3.tok])
            K.op("dve", lambda e: e.scalar_tensor_tensor(st2.ap[P64, :], st2.ap[P64, :], 1.0 / 64, st3.ap[P64, :], ALU.mult,
                                                         ALU.subtract), reads=[st2.tok, st3.tok], writes=[st2.tok])
            K.op("act", lambda e: e.activation(out=st2.ap[P64, :], in_=st2.ap[P64, :], func=AF.Sqrt, bias=GN_EPS),
                 reads=[st2.tok], writes=[st2.tok])
            K.op("dve", lambda e: e.reciprocal(st3.ap[P64, :], st2.ap[P64, :]), reads=[st2.tok, st3.tok], writes=[st3.tok])
            yield
            K.op("dve", lambda e: e.tensor_tensor(Yv, Yv, st1.ap[P64, :].unsqueeze(2).to_broadcast([64, NCH * 8, 64]), ALU.subtract),
                 reads=[Y.tok, st1.tok], writes=[Y.tok])
            K.op("pool", lambda e: e.tensor_tensor(Yv, Yv, st3.ap[P64, :].unsqueeze(2).to_broadcast([64, NCH * 8, 64]), ALU.mult),
                 reads=[Y.tok, st3.tok], writes=[Y.tok])
            yield
            for hp in range(4):
                bk = bank()
                for c in range(NCH):
                    K.op("pe", lambda e: e.transpose(bk.ap[:, c * 64:(c + 1) * 64], Y.ap[P64, c, hp * 128:(hp + 1) * 128],
                                                     ident.ap[0:64, 0:64]),
                         reads=[Y.tok], writes=[bk.tok], signal=(c == NCH - 1))
                o_ = ostg[osi[0] % 2]
                yield
                K.op("dve", lambda e: e.tensor_scalar(to1.ap, bk.ap[:, 0:RW], pcol(l, "gng", hp), pcol(l, "gnb", hp),
                                                      ALU.mult, ALU.add), reads=[bk.tok], writes=[to1.tok])
                K.op("pool", lambda e: e.tensor_tensor(to1.ap, to1.ap, bonus.ap[:, hp, :], ALU.add), reads=[to1.tok, bonus.tok],
                     writes=[to1.tok])
                K.op("pool", lambda e: e.tensor_tensor(o_.ap, to1.ap, gall.ap[:, hp, :], ALU.mult), reads=[to1.tok, gall.tok],
                     writes=[o_.tok])
                K.dma("sp", yall[1024 + hp * 128:1024 + (hp + 1) * 128, t0:t0 + RW], o_.ap, reads=[o_.tok],
                      sem="st%d" % (osi[0] % 2))
                osi[0] += 1
                yield

        blocks = [(b, blk) for b in range(NB) for blk in range(S // RW)]
        prev = None
        for i, (b, blk) in enumerate(blocks):
            st = sets[i % 2]
            gB = genB(prev[0], prev[1], prev[2]) if prev is not None else None
            interleave(genA(b, blk, st), gB)
            prev = (b, blk, st)
        interleave(genB(prev[0], prev[1], prev[2]))
        K.barrier()

    STAGE_RWKV = [stage_rwkv]

    def scoped(name, fn, *a):
        with nc.named_scope(name):
            fn(*a)

    def run():
        if stop_after == ("pre", 0):
            return
        for l in range(nlayers):
            xsrc = xT if l == 0 else xLd
            scoped('inproj%d' % l, stage_inproj, l, xsrc)
            if stop_after == ("inproj", l):
                return
            import os
            if os.environ.get("KDBG_SKIP"):
                stage_merge(l, xsrc)
                stage_ple(l)
                stage_ffn(l, outT if l == nlayers - 1 else xLd)
                continue
            scoped('rglru%d' % l, stage_rglru, l)
            scoped('sconv%d' % l, stage_sconv, l)
            if stop_after == ("ab", l):
                return
            scoped('attn%d' % l, stage_attn, l)
            if stop_after == ("attn", l):
                return
            scoped('rwkv%d' % l, STAGE_RWKV[0], l)
            if stop_after == ("rwkv", l):
                return
            scoped('merge%d' % l, stage_merge, l, xsrc)
            if stop_after == ("merge", l):
                return
            scoped('ple%d' % l, stage_ple, l)
            scoped('ffn%d' % l, stage_ffn, l, outT if l == nlayers - 1 else xLd)

    run()
    K.barrier()
    print('[kernel] instructions:', K.ninstr, 'sems:', len(K.sems))
    es.close()
    return nc


def _consts():
    c = {}
    c["c_ident"] = np.eye(128, dtype=np.float32)
    ob = np.zeros((128, 128), np.float32)
    ob[0:64, 0:64] = 1.0
    ob[64:128, 64:128] = 1.0
    c["c_onesbd"] = ob
    j = np.arange(64)[:, None]
    s = np.arange(64)[None, :]
    c["c_mask1"] = np.concatenate([(j < s), (j <= s)], axis=1).astype(np.float32)
    c["c_masksl"] = (s < j).astype(np.float32)
    rm = np.ones((128, 512), np.float32)
    rm[:, 0::64] = 0.0
    c["c_rmask"] = rm
    return c


def _bias_table(rel_bias):
    ki = np.arange(128)[:, None]
    qi = np.arange(128)[None, :]
    tab = np.empty((128, 8, 5, 128), np.float32)
    for idx in range(5):
        rel = (4 - idx) * 128 + qi - ki
        g = rel_bias[:, np.clip(rel, -128, 128) + 128]
        if idx == 4:
            g = np.where(((ki >= 64) & (qi < 64))[None], np.float32(-30000.0), g)
        if idx == 0:
            g = np.where(((ki < 64) & (qi >= 64))[None], np.float32(-30000.0), g)
        tab[:, :, idx, :] = g.transpose(1, 0, 2)
    return np.ascontiguousarray(tab.reshape(128, 8 * 5 * 128))


def _pvec(inp):
    out = np.zeros((2, 128, NPV), np.float32)
    for l in range(2):
        P = out[l]

        def put(name, flat):
            n = flat.size // 128
            P[:, PV[name]:PV[name] + n] = flat.reshape(n, 128).T

        P[:, PV["cw"]:PV["cw"] + 16] = inp["lru_conv_w"][l].T.reshape(4, 128, 4).transpose(1, 0, 2).reshape(128, 16)
        put("cb", inp["lru_conv_b"][l]); put("br", inp["lru_br"][l]); put("bi", inp["lru_bi"][l])
        put("lam", inp["lru_lambda"][l])
        P[:, PV["sw"]:PV["sw"] + 12] = inp["sconv_w"][l].T.reshape(4, 128, 3).transpose(1, 0, 2).reshape(128, 12)
        put("mu", inp["rwkv_mu"][l]); put("w0", inp["rwkv_w0"][l]); put("a0", inp["rwkv_a0"][l])
        put("kk", inp["rwkv_k_k"][l]); put("ka", inp["rwkv_k_a"][l]); put("rk", inp["rwkv_r_k"][l].reshape(-1))
        put("gng", inp["rwkv_gn_g"][l]); put("gnb", inp["rwkv_gn_b"][l])
        put("bg", inp["b_gate"][l].reshape(-1)); put("l1g", inp["ln1_g"][l]); put("l1b", inp["ln1_b"][l])
        put("bpg", inp["b_ple_gate"][l]); put("l2g", inp["ln2_g"][l]); put("l2b", inp["ln2_b"][l])
    return out


def _shared_inputs(inp):
    f = lambda a: np.ascontiguousarray(np.asarray(a, dtype=np.float32))
    sh = {
        "w_in": f(inp["w_in"]),
        "w_gate": f(np.asarray(inp["w_gate"]).transpose(0, 2, 1, 3).reshape(2, DM, 4096)),
        "w_branch": f(inp["w_branch"]), "w_out": f(inp["w_out"]), "w_ff1": f(inp["w_ff1"]), "w_ff2": f(inp["w_ff2"]),
        "w_ple": f(inp["w_ple"]), "w_pg": f(inp["w_ple_gate"]), "lru_wr": f(inp["lru_wr"]), "lru_wi": f(inp["lru_wi"]),
        "rw2": f(inp["rwkv_w2"]), "ra2": f(inp["rwkv_a2"]), "rg2": f(inp["rwkv_g2"]),
        "pvec": _pvec({k: np.asarray(v, dtype=np.float32) for k, v in inp.items()}),
        "muv": f(np.broadcast_to(np.asarray(inp["rwkv_mu"], dtype=np.float32)[:, None, 1024:1536], (2, 64, 512))),
        "c_bias": _bias_table(np.asarray(inp["rel_bias"], dtype=np.float32)),
    }
    sh.update(_consts())
    return sh


def _core_inputs(inp, b0, nb):
    x = np.asarray(inp["x"], dtype=np.float32)[b0:b0 + nb]
    p = np.asarray(inp["p"], dtype=np.float32)[:, b0:b0 + nb]
    xT = np.ascontiguousarray(x.reshape(nb * S, DM).T)
    pT = np.ascontiguousarray(p.reshape(2, nb * S, 256).transpose(0, 2, 1))
    return {"xT": xT, "pT": pT}


def kernel(**inputs):
    ncores = 8
    nb = 4
    nc = build_program(nb)
    sh = _shared_inputs(inputs)
    in_maps = []
    for c in range(ncores):
        m = dict(sh)
        m.update(_core_inputs(inputs, c * nb, nb))
        in_maps.append(m)
    res = run_bass_kernel_spmd(nc, in_maps, core_ids=list(range(ncores)))
    out = np.empty((ncores * nb, S, DM), np.float32)
    for c in range(ncores):
        oT = res.results[c]["outT"]
        out[c * nb:(c + 1) * nb] = oT.T.reshape(nb, S, DM)
    return out
```

```python
import numpy as np
from contextlib import ExitStack
import concourse.bass as bass
import concourse.mybir as mybir
from concourse.bass_utils import run_bass_kernel_spmd

F32 = mybir.dt.float32
F32R = mybir.dt.float32r
BF16 = mybir.dt.bfloat16
U8 = mybir.dt.uint8
ALU = mybir.AluOpType
AF = mybir.ActivationFunctionType
AX = mybir.AxisListType

S = 2048
DM = 1024
NCOL = 5888
ALPHA = 4.0 ** 0.25
C0 = float(np.exp(-0.5))
GN_EPS = 64 * 1e-5
LN_EPS = 1e-5

PV = {}
_o = 0
for _n, _w in (("cw", 16), ("cb", 4), ("br", 4), ("bi", 4), ("lam", 4), ("sw", 12), ("mu", 14), ("w0", 4),
               ("a0", 4), ("kk", 4), ("ka", 4), ("rk", 4), ("gng", 4), ("gnb", 4), ("bg", 32), ("l1g", 8),
               ("l1b", 8), ("bpg", 8), ("l2g", 8), ("l2b", 8), ("omka", 4), ("clam", 4)):
    PV[_n] = _o
    _o += _w
NPV = _o


class Tok:
    __slots__ = ("w", "r")

    def __init__(self):
        self.w = None
        self.r = []


class Tile:
    __slots__ = ("ap", "tok")

    def __init__(self, ap, tok=None):
        self.ap = ap
        self.tok = tok or Tok()


class KB:
    def __init__(self, nc, es):
        self.nc = nc
        self.es = es
        self.eng = {"pe": nc.tensor, "act": nc.scalar, "dve": nc.vector, "pool": nc.gpsimd, "sp": nc.sync}
        self.sems = {}
        self.cnt = {}
        self.seen = {e: {} for e in self.eng}
        for e in self.eng:
            self._sem("c_" + e)
        self.ninstr = 0

    def _sem(self, name):
        if name not in self.sems:
            self.sems[name] = self.es.enter_context(self.nc.semaphore(name))
            self.cnt[name] = 0
        return self.sems[name]

    def _deps(self, reads, writes):
        deps = []
        for t in reads:
            if t.w is not None:
                deps.append(t.w)
        for t in writes:
            if t.w is not None:
                deps.append(t.w)
            deps.extend(t.r)
        return deps

    def _wait(self, eng, deps):
        best = {}
        for (s, v) in deps:
            if eng == "pe" and s == "c_pe":
                continue
            if self.seen[eng].get(s, 0) >= v:
                continue
            if best.get(s, 0) < v:
                best[s] = v
        for s, v in best.items():
            self.eng[eng].wait_ge(self.sems[s], v)
            self.seen[eng][s] = v
            self.ninstr += 1

    def _reg(self, ev, reads, writes):
        for t in reads:
            t.r = [e for e in t.r if e[0] != ev[0]]
            t.r.append(ev)
        for t in writes:
            t.w = ev
            t.r = []

    def op(self, eng, fn, reads=(), writes=(), signal=True):
        self._wait(eng, self._deps(reads, writes))
        ins = fn(self.eng[eng])
        self.ninstr += 1
        s = "c_" + eng
        if signal:
            self.cnt[s] += 1
            ins.then_inc(self.sems[s], 1)
            ev = (s, self.cnt[s])
        else:
            ev = (s, self.cnt[s] + 1)
        self._reg(ev, reads, writes)

    def dma(self, q, out, in_, reads=(), writes=(), sem="dma"):
        sem = sem + "@" + q
        self._sem(sem)
        self._wait(q, self._deps(reads, writes))
        ins = self.eng[q].dma_start(out=out, in_=in_)
        self.ninstr += 1
        self.cnt[sem] += 16
        ins.then_inc(self.sems[sem], 16)
        self._reg((sem, self.cnt[sem]), reads, writes)

    def barrier(self, engines=None):
        for e in (engines or self.eng):
            for s, v in self.cnt.items():
                if v > 0 and self.seen[e].get(s, 0) < v:
                    self.eng[e].wait_ge(self.sems[s], v)
                    self.seen[e][s] = v
                    self.ninstr += 1


class Arena:
    def __init__(self, ap, nbytes):
        self.ap = ap
        self.nbytes = nbytes
        self.off = 0
        self.base = 0

    def alloc(self, shape, dt, parts=128, at=None):
        if at is not None:
            save = self.off
            self.off = at
            t = self.alloc(shape, dt, parts)
            self.off = save
            return t
        self.last = self.off
        esz = 4 if dt == F32 else 2
        n = int(np.prod(shape))
        nb = (n * esz + 31) // 32 * 32
        assert self.off + nb <= self.nbytes, ("SBUF arena overflow", self.off, nb)
        a = self.ap[0:parts, self.off:self.off + nb]
        if nb != n * esz:
            a = self.ap[0:parts, self.off:self.off + n * esz]
        a = a.bitcast(dt)
        self.off += nb
        if len(shape) == 2:
            a = a.rearrange("p (a b) -> p a b", a=shape[0])
        elif len(shape) == 3:
            a = a.rearrange("p (a b c) -> p a b c", a=shape[0], b=shape[1])
        elif len(shape) == 4:
            a = a.rearrange("p (a b c d) -> p a b c d", a=shape[0], b=shape[1], c=shape[2])
        return Tile(a)

    def mark(self):
        self.base = self.off

    def reset(self):
        self.off = self.base


def build_program(NB, stop_after=None, dbg=False, nlayers=2):
    T = NB * S
    nc = bass.Bass("TRN2", target_bir_lowering=False)
    es = ExitStack()
    K = KB(nc, es)

    def din(name, shape):
        return nc.dram_tensor(name, list(shape), F32, kind="ExternalInput").ap()

    def dscr(name, shape, dt=F32):
        return nc.dram_tensor(name, list(shape), dt, kind=("ExternalOutput" if dbg else "Internal")).ap()

    xT = din("xT", [DM, T])
    pT = din("pT", [2, 256, T])
    w_in = din("w_in", [2, DM, NCOL])
    w_gate = din("w_gate", [2, DM, 4096])
    w_branch = din("w_branch", [2, 4, 512, DM])
    w_out = din("w_out", [2, DM, DM])
    w_ff1 = din("w_ff1", [2, DM, 4096])
    w_ff2 = din("w_ff2", [2, 4096, DM])
    w_ple = din("w_ple", [2, 256, DM])
    w_pg = din("w_pg", [2, DM, DM])
    lru_wr = din("lru_wr", [2, 8, 64, 64])
    lru_wi = din("lru_wi", [2, 8, 64, 64])
    rw2 = din("rw2", [2, 64, 512])
    ra2 = din("ra2", [2, 64, 512])
    rg2 = din("rg2", [2, 128, 512])
    pvec = din("pvec", [2, 128, NPV])
    muv = din("muv", [2, 64, 512])
    c_ident = din("c_ident", [128, 128])
    c_onesbd = din("c_onesbd", [128, 128])
    c_mask1 = din("c_mask1", [64, 128])
    c_masksl = din("c_masksl", [64, 64])
    c_rmask = din("c_rmask", [128, 512])
    c_bias = din("c_bias", [128, 8 * 5 * 128])
    outT = nc.dram_tensor("outT", [DM, T], F32, kind="ExternalOutput").ap()

    hFM = dscr("hFM", [NCOL, T])
    vTM = dscr("vTM", [T, 1024])
    yall = dscr("yall", [2048, T], BF16)
    x1d = dscr("x1d", [DM, T])
    xLd = dscr("xLd", [DM, T])
    based = dscr("based", [DM, T])
    mrgd = dscr("mrgd", [DM, T], BF16)

    arena_t = es.enter_context(nc.sbuf_tensor("arena", [128, 207 * 1024], U8))
    A = Arena(arena_t, 207 * 1024)
    psum_t = es.enter_context(nc.psum_tensor("psum", [128, 4096], F32))
    banks = [Tile(psum_t[:, i * 512:(i + 1) * 512]) for i in range(8)]
    bank_i = [0]

    def bank():
        b = banks[bank_i[0] % 8]
        bank_i[0] += 1
        return b

    pv = [A.alloc([NPV], F32) for _ in range(2)]
    ident = A.alloc([128], F32)
    identb = A.alloc([128], BF16)
    onesbd = A.alloc([128], F32)
    onesb = A.alloc([64], BF16)
    onesf = A.alloc([128], F32)
    mask1 = A.alloc([128], F32)
    masksl = A.alloc([64], F32)
    rmask = A.alloc([512], F32)
    ctok = Tok()
    for l in range(2):
        K.dma("sp", pv[l].ap, pvec[l], writes=[ctok], sem="ld_c")
    K.dma("sp", ident.ap, c_ident[:, :], writes=[ctok], sem="ld_c")
    K.dma("pool", identb.ap, c_ident[:, :], writes=[ctok], sem="ld_c")
    K.dma("sp", onesbd.ap, c_onesbd[:, :], writes=[ctok], sem="ld_c")
    K.dma("sp", mask1.ap[0:64, :], c_mask1[:, :], writes=[ctok], sem="ld_c")
    K.dma("sp", mask1.ap[64:128, :], c_mask1[:, :], writes=[ctok], sem="ld_c")
    K.dma("sp", masksl.ap[0:64, :], c_masksl[:, :], writes=[ctok], sem="ld_c")
    K.dma("sp", rmask.ap, c_rmask[:, :], writes=[ctok], sem="ld_c")
    K.op("dve", lambda e: e.memset(onesb.ap, 1.0), writes=[ctok])
    K.op("dve", lambda e: e.memset(onesf.ap, 1.0), writes=[ctok])
    for l in range(2):
        p = pv[l].ap
        K.op("dve", lambda e, p=p: e.tensor_scalar(p[:, PV["omka"]:PV["omka"] + 4], p[:, PV["ka"]:PV["ka"] + 4],
                                                   -1.0, 1.0, ALU.mult, ALU.add), reads=[ctok], writes=[ctok])
        K.op("act", lambda e, p=p: e.activation(out=p[:, PV["clam"]:PV["clam"] + 4], in_=p[:, PV["lam"]:PV["lam"] + 4],
                                                func=AF.Exp, scale=-1.0), reads=[ctok], writes=[ctok])
        K.op("act", lambda e, p=p: e.activation(out=p[:, PV["clam"]:PV["clam"] + 4], in_=p[:, PV["clam"]:PV["clam"] + 4],
                                                func=AF.Ln, bias=1.0), reads=[ctok], writes=[ctok])
        K.op("act", lambda e, p=p: e.mul(p[:, PV["clam"]:PV["clam"] + 4], p[:, PV["clam"]:PV["clam"] + 4], -8.0),
             reads=[ctok], writes=[ctok])
    K.barrier()
    A.mark()

    def pcol(l, name, j):
        c = PV[name] + j
        return pv[l].ap[:, c:c + 1]

    evi = [0]

    def evac_eng():
        evi[0] += 1
        return "act" if evi[0] % 2 else "dve"

    def copy_op(eng, out, in_):
        if eng == "act":
            return lambda e: e.copy(out, in_)
        return lambda e: e.tensor_copy(out, in_)

    def stage_inproj(l, xsrc):
        A.reset()
        wsb = A.alloc([8, NCOL], BF16)
        for kc in range(8):
            K.dma("pool", wsb.ap[:, kc, :], w_in[l, kc * 128:(kc + 1) * 128, :], writes=[wsb.tok], sem="ld_w")
        xb = [A.alloc([8, 512], BF16) for _ in range(2)]
        stg = [A.alloc([512], F32) for _ in range(6)]
        si = 0
        xs3 = xsrc.rearrange("(kc p) t -> p kc t", p=128)
        import os
        _ntb = int(os.environ.get("KDBG_TB", T // 512)); _noc = int(os.environ.get("KDBG_OC", 46)); _ntm = int(os.environ.get("KDBG_TM", 4))
        for tb in range(_ntb):
            xs = xb[tb % 2]
            K.dma("pool", xs.ap, xs3[:, :, tb * 512:(tb + 1) * 512], writes=[xs.tok], sem="ld_x%d" % (tb % 2))
            for oc in range(_noc):
                if oc >= 42:
                    continue
                bk = bank()
                for kc in range(8):
                    K.op("pe", lambda e, kc=kc, oc=oc, bk=bk, xs=xs: e.matmul(
                        bk.ap, wsb.ap[:, kc, oc * 128:(oc + 1) * 128], xs.ap[:, kc, :], start=(kc == 0), stop=(kc == 7)),
                        reads=[wsb.tok, xs.tok], writes=[bk.tok], signal=(kc == 7))
                st = stg[si % 6]
                K.op(evac_eng(), copy_op("act" if evi[0] % 2 else "dve", st.ap, bk.ap), reads=[bk.tok], writes=[st.tok])
                K.dma("sp", hFM[oc * 128:(oc + 1) * 128, tb * 512:(tb + 1) * 512], st.ap, reads=[st.tok],
                      sem="st%d" % (si % 6))
                si += 1
            for (c0, d0) in ((3584, 0), (5376, 512)):
                for tt in range(_ntm):
                    bk = bank()
                    for kc in range(8):
                        K.op("pe", lambda e, kc=kc, bk=bk, xs=xs, tt=tt, c0=c0: e.matmul(
                            bk.ap, xs.ap[:, kc, tt * 128:(tt + 1) * 128], wsb.ap[:, kc, c0:c0 + 512],
                            start=(kc == 0), stop=(kc == 7)),
                            reads=[wsb.tok, xs.tok], writes=[bk.tok], signal=(kc == 7))
                    st = stg[si % 6]
                    K.op(evac_eng(), copy_op("act" if evi[0] % 2 else "dve", st.ap, bk.ap), reads=[bk.tok], writes=[st.tok])
                    r0 = tb * 512 + tt * 128
                    K.dma("sp", vTM[r0:r0 + 128, d0:d0 + 512], st.ap, reads=[st.tok], sem="st%d" % (si % 6))
                    si += 1
        K.barrier()

    def interleave(*gens):
        gens = [g for g in gens if g is not None]
        while gens:
            for g in list(gens):
                try:
                    next(g)
                except StopIteration:
                    gens.remove(g)

    def stage_rglru(l):
        A.reset()
        wr = [A.alloc([128], F32) for _ in range(4)]
        wi = [A.alloc([128], F32) for _ in range(4)]
        wtok = Tok()
        for cc in range(4):
            for w_t, src in ((wr[cc], lru_wr), (wi[cc], lru_wi)):
                K.op("dve", lambda e, w_t=w_t: e.memset(w_t.ap, 0.0), writes=[wtok])
                K.dma("sp", w_t.ap[0:64, 0:64], src[l, 2 * cc, :, :], writes=[wtok], sem="ld_w")
                K.dma("sp", w_t.ap[64:128, 64:128], src[l, 2 * cc + 1, :, :], writes=[wtok], sem="ld_w")
        nb = 2
        xa = [A.alloc([S + 3], F32) for _ in range(nb)]
        ya = [A.alloc([S], F32) for _ in range(nb)]
        for t_ in xa:
            K.op("dve", lambda e, t_=t_: e.memset(t_.ap[:, 0:3], 0.0), writes=[t_.tok])
        xc2 = [A.alloc([S], F32) for _ in range(2)]
        rr2 = [A.alloc([S], F32) for _ in range(2)]
        ii2 = [A.alloc([S], F32) for _ in range(2)]
        aa2 = [A.alloc([S], F32) for _ in range(2)]
        mm2 = [A.alloc([S], F32) for _ in range(2)]
        hh2 = [A.alloc([S], F32) for _ in range(2)]
        gg2 = [A.alloc([S], F32) for _ in range(2)]
        oo = [A.alloc([S], BF16) for _ in range(2)]
        def unit(it, cc, b):
            if True:
                x_ = xa[it % nb]
                y_ = ya[it % nb]
                o_ = oo[it % 2]
                xc, rr, ii, aa, mm, hh, gg = (xc2[it % 2], rr2[it % 2], ii2[it % 2], aa2[it % 2], mm2[it % 2], hh2[it % 2],
                                              gg2[it % 2])
                uu = ii
                K.dma("sp", x_.ap[:, 3:], hFM[cc * 128:(cc + 1) * 128, b * S:(b + 1) * S], writes=[x_.tok],
                      sem="ld_x%d" % (it % nb))
                K.dma("sp", y_.ap, hFM[512 + cc * 128:512 + (cc + 1) * 128, b * S:(b + 1) * S], writes=[y_.tok],
                      sem="ld_y%d" % (it % nb))
                cw = lambda k: pcol(l, "cw", cc * 4 + k)
                K.op("act", lambda e: e.activation(out=xc.ap, in_=x_.ap[:, 0:S], func=AF.Identity, scale=cw(0),
                                                   bias=pcol(l, "cb", cc)), reads=[x_.tok], writes=[xc.tok])
                for k in range(1, 4):
                    K.op("dve", lambda e, k=k: e.scalar_tensor_tensor(xc.ap, x_.ap[:, k:k + S], cw(k), xc.ap, ALU.mult, ALU.add),
                         reads=[x_.tok, xc.tok], writes=[xc.tok])
                yield
                for q in range(4):
                    sl = slice(q * 512, (q + 1) * 512)
                    yield
                    for (w_t, dst, bn) in ((wr[cc], rr, "br"), (wi[cc], ii, "bi")):
                        bk = bank()
                        K.op("pe", lambda e, bk=bk, w_t=w_t, sl=sl: e.matmul(bk.ap, w_t.ap, xc.ap[:, sl], start=True, stop=True),
                             reads=[wtok, xc.tok], writes=[bk.tok])
                        K.op("act", lambda e, bk=bk, dst=dst, sl=sl, bn=bn: e.activation(
                            out=dst.ap[:, sl], in_=bk.ap, func=AF.Sigmoid, bias=pcol(l, bn, cc)),
                            reads=[bk.tok], writes=[dst.tok])
                yield
                K.op("act", lambda e: e.activation(out=aa.ap, in_=rr.ap, func=AF.Exp, scale=pcol(l, "clam", cc)),
                     reads=[rr.tok], writes=[aa.tok])
                K.op("act", lambda e: e.activation(out=mm.ap, in_=aa.ap, func=AF.Square), reads=[aa.tok], writes=[mm.tok])
                K.op("act", lambda e: e.activation(out=mm.ap, in_=mm.ap, func=AF.Sqrt, scale=-1.0, bias=1.0),
                     reads=[mm.tok], writes=[mm.tok])
                yield
                K.op("dve", lambda e: e.tensor_tensor(uu.ap, ii.ap, xc.ap, ALU.mult), reads=[ii.tok, xc.tok], writes=[uu.tok])
                K.op("dve", lambda e: e.tensor_tensor(uu.ap, uu.ap, mm.ap, ALU.mult), reads=[uu.tok, mm.tok], writes=[uu.tok])
                K.op("dve", lambda e: e.tensor_tensor_scan(hh.ap, aa.ap, uu.ap, 0.0, ALU.mult, ALU.add),
                     reads=[aa.tok, uu.tok], writes=[hh.tok])
                yield
                K.op("act", lambda e: e.activation(out=gg.ap, in_=y_.ap, func=AF.Square), reads=[y_.tok], writes=[gg.tok])
                K.op("dve", lambda e: e.tensor_scalar(gg.ap, gg.ap, 0.044715, 1.0, ALU.mult, ALU.add),
                     reads=[gg.tok], writes=[gg.tok])
                K.op("dve", lambda e: e.tensor_tensor(gg.ap, gg.ap, y_.ap, ALU.mult), reads=[gg.tok, y_.tok], writes=[gg.tok])
                yield
                K.op("act", lambda e: e.activation(out=gg.ap, in_=gg.ap, func=AF.Sigmoid, scale=1.5957691216),
                     reads=[gg.tok], writes=[gg.tok])
                K.op("dve", lambda e: e.tensor_tensor(gg.ap, gg.ap, y_.ap, ALU.mult), reads=[gg.tok, y_.tok], writes=[gg.tok])
                K.op("dve", lambda e: e.tensor_tensor(o_.ap, hh.ap, gg.ap, ALU.mult), reads=[hh.tok, gg.tok], writes=[o_.tok])
                K.dma("sp", yall[cc * 128:(cc + 1) * 128, b * S:(b + 1) * S], o_.ap, reads=[o_.tok], sem="st%d" % (it % 2))
                yield

        units = [(cc, b) for cc in range(4) for b in range(NB)]
        for k in range(0, len(units), 2):
            interleave(unit(k, *units[k]), unit(k + 1, *units[k + 1]) if k + 1 < len(units) else None)
        K.barrier()

    def stage_sconv(l):
        A.reset()
        nb = 2
        bgt = [A.alloc([S], F32) for _ in range(nb)]
        cgt = [A.alloc([S], F32) for _ in range(nb)]
        xht = [A.alloc([S], F32) for _ in range(nb)]
        tt2 = [A.alloc([S + 2], F32) for _ in range(2)]
        for t_ in tt2:
            K.op("dve", lambda e: e.memset(t_.ap[:, 0:2], 0.0), writes=[t_.tok])
        yy2 = [A.alloc([S], F32) for _ in range(2)]
        oo = [A.alloc([S], BF16) for _ in range(2)]
        def unit(it, cc, b):
            if True:
                i_ = it % nb
                for (t_, r0, nm) in ((bgt[i_], 1024, "a"), (cgt[i_], 1536, "b"), (xht[i_], 2048, "c")):
                    K.dma("sp", t_.ap, hFM[r0 + cc * 128:r0 + (cc + 1) * 128, b * S:(b + 1) * S], writes=[t_.tok],
                          sem="ld_%s%d" % (nm, i_))
                o_ = oo[it % 2]
                tt_, yy = tt2[it % 2], yy2[it % 2]
                sw = lambda k: pcol(l, "sw", cc * 3 + k)
                K.op("dve", lambda e: e.tensor_tensor(tt_.ap[:, 2:], cgt[i_].ap, xht[i_].ap, ALU.mult),
                     reads=[cgt[i_].tok, xht[i_].tok], writes=[tt_.tok])
                yield
                K.op("act", lambda e: e.activation(out=yy.ap, in_=tt_.ap[:, 0:S], func=AF.Copy, scale=sw(0)),
                     reads=[tt_.tok], writes=[yy.tok])
                for k in (1, 2):
                    K.op("dve", lambda e, k=k: e.scalar_tensor_tensor(yy.ap, tt_.ap[:, k:k + S], sw(k), yy.ap, ALU.mult, ALU.add),
                         reads=[tt_.tok, yy.tok], writes=[yy.tok])
                K.op("dve", lambda e: e.tensor_tensor(o_.ap, yy.ap, bgt[i_].ap, ALU.mult), reads=[yy.tok, bgt[i_].tok],
                     writes=[o_.tok])
                yield
                K.dma("sp", yall[512 + cc * 128:512 + (cc + 1) * 128, b * S:(b + 1) * S], o_.ap, reads=[o_.tok],
                      sem="st%d" % (it % 2))
                yield

        units = [(cc, b) for cc in range(4) for b in range(NB)]
        for k in range(0, len(units), 2):
            interleave(unit(k, *units[k]), unit(k + 1, *units[k + 1]) if k + 1 < len(units) else None)
        K.barrier()

    def stage_attn(l):
        A.reset()
        bias = A.alloc([8, 5, 128], F32)
        K.dma("sp", bias.ap, c_bias.rearrange("p (h v k) -> p h v k", h=8, v=5), writes=[bias.tok], sem="ld_w")
        qb2 = [A.alloc([4, S], BF16) for _ in range(2)]
        kb2 = [A.alloc([4, S], BF16) for _ in range(2)]
        vb2 = [A.alloc([16, 512], BF16) for _ in range(2)]
        vaug = A.alloc([16, 8, 128], BF16)
        of = A.alloc([4, S], BF16)
        sT = [A.alloc([640], F32) for _ in range(2)]
        pTt = [A.alloc([640], BF16) for _ in range(2)]
        rd = [A.alloc([128], F32) for _ in range(2)]
        psS = [Tile(psum_t[:, 0:1024]), Tile(psum_t[:, 1024:2048])]
        psO = [Tile(psum_t[:, 2048:2560]), Tile(psum_t[:, 2560:3072])]
        K.op("pool", lambda e: e.memset(vaug.ap, 1.0), writes=[vaug.tok])
        va5 = vaug.ap.rearrange("p n (a b) k -> p n a b k", a=4)

        def loads(b):
            cs_ = slice(b * S, (b + 1) * S)
            q_, k_, v_ = qb2[b % 2], kb2[b % 2], vb2[b % 2]
            K.dma("pool", q_.ap, hFM[4352:4864, cs_].rearrange("(c p) t -> p c t", p=128), writes=[q_.tok], sem="ld_q%d" % (b % 2))
            K.dma("pool", k_.ap, hFM[4864:5376, cs_].rearrange("(c p) t -> p c t", p=128), writes=[k_.tok], sem="ld_k%d" % (b % 2))
            K.dma("pool", v_.ap, vTM[cs_, 512:1024].rearrange("(n p) f -> p n f", p=128), writes=[v_.tok], sem="ld_v%d" % (b % 2))

        loads(0)
        for b in range(NB):
            cs = slice(b * S, (b + 1) * S)
            qb, kb_, vb = qb2[b % 2], kb2[b % 2], vb2[b % 2]
            vb5 = vb.ap.rearrange("p n (a b d) -> p n a b d", a=4, b=2)
            if b + 1 < NB:
                loads(b + 1)
            for n4 in range(4):
                ns = slice(n4 * 4, n4 * 4 + 4)
                K.op("act", lambda e: e.copy(va5[:, ns, :, 0, 0:64], vb5[:, ns, :, 0, :]), reads=[vb.tok], writes=[vaug.tok])
                K.op("dve", lambda e: e.tensor_copy(va5[:, ns, :, 1, 64:128], vb5[:, ns, :, 1, :]), reads=[vb.tok], writes=[vaug.tok])
            units = [(h, j) for j in range(16) for h in range(8)]

            def front(it, h, j):
                hr = slice((h % 2) * 64, (h % 2) * 64 + 64)
                hc = h // 2
                nkb = min(j, 4) + 1
                i0_ = 5 - nkb
                ps_s, s_, p_ = psS[it % 2], sT[it % 2], pTt[it % 2]
                for i in range(nkb):
                    kbi = j - nkb + 1 + i
                    K.op("pe", lambda e: e.matmul(ps_s.ap[:, i * 128:(i + 1) * 128], kb_.ap[hr, hc, kbi * 128:(kbi + 1) * 128],
                                                  qb.ap[hr, hc, j * 128:(j + 1) * 128], start=True, stop=True),
                         reads=[qb.tok, kb_.tok], writes=[ps_s.tok], signal=(i == nkb - 1))
                w = nkb * 128
                K.op("dve", lambda e: e.scalar_tensor_tensor(
                    s_.ap[:, 0:w], ps_s.ap[:, 0:w], 0.125, bias.ap[:, h, i0_:5, :].rearrange("p v k -> p (v k)"),
                    ALU.mult, ALU.add), reads=[ps_s.tok, bias.tok], writes=[s_.tok])
                K.op("act", lambda e: e.activation(out=p_.ap[:, 0:w], in_=s_.ap[:, 0:w], func=AF.Exp),
                     reads=[s_.tok], writes=[p_.tok])

            def back(it, h, j):
                hr = slice((h % 2) * 64, (h % 2) * 64 + 64)
                ho = slice(64 - (h % 2) * 64, 128 - (h % 2) * 64)
                hc = h // 2
                nkb = min(j, 4) + 1
                ps_o, p_, r_ = psO[it % 2], pTt[it % 2], rd[it % 2]
                for i in range(nkb):
                    kbi = j - nkb + 1 + i
                    K.op("pe", lambda e: e.matmul(ps_o.ap[:, 0:128], vaug.ap[:, kbi, h, :], p_.ap[:, i * 128:(i + 1) * 128],
                                                  start=(i == 0), stop=(i == nkb - 1)),
                         reads=[vaug.tok, p_.tok], writes=[ps_o.tok], signal=(i == nkb - 1))
                K.op("dve", lambda e: e.reciprocal(r_.ap[hr, :], ps_o.ap[ho, 0:128]), reads=[ps_o.tok], writes=[r_.tok])
                K.op("dve", lambda e: e.tensor_tensor(of.ap[hr, hc, j * 128:(j + 1) * 128], ps_o.ap[hr, 0:128],
                                                      r_.ap[hr, :], ALU.mult),
                     reads=[ps_o.tok, r_.tok], writes=[of.tok])

            for it, (h, j) in enumerate(units):
                front(it, h, j)
                if it > 0:
                    back(it - 1, *units[it - 1])
            back(len(units) - 1, *units[-1])
            K.dma("sp", yall[1536:2048, cs].rearrange("(c p) t -> p c t", p=128), of.ap, reads=[of.tok], sem="st0")
        K.barrier()

    def layernorm_fm(z, zsq, tmp, l, gname, bname, TBW, dst, t0, stg, sti):
        yield
        K.op("act", lambda e: e.activation(out=zsq.ap, in_=z.ap, func=AF.Square), reads=[z.tok], writes=[zsq.tok])
        yield
        yield
        bM = bank()
        bQ = bank()
        for oc in range(8):
            K.op("pe", lambda e, oc=oc: e.matmul(bM.ap[:, 0:TBW], onesf.ap, z.ap[:, oc, :], start=(oc == 0), stop=(oc == 7)),
                 reads=[z.tok], writes=[bM.tok], signal=(oc == 7))
        for oc in range(8):
            K.op("pe", lambda e, oc=oc: e.matmul(bQ.ap[:, 0:TBW], onesf.ap, zsq.ap[:, oc, :], start=(oc == 0), stop=(oc == 7)),
                 reads=[zsq.tok], writes=[bQ.tok], signal=(oc == 7))
        yield
        mean = Tile(tmp.ap[:, 0, :], tmp.tok)
        msq = Tile(tmp.ap[:, 1, :], tmp.tok)
        rstd = Tile(tmp.ap[:, 2, :], tmp.tok)
        K.op("dve", lambda e: e.tensor_scalar(mean.ap, bM.ap[:, 0:TBW], 1.0 / DM, None, ALU.mult), reads=[bM.tok], writes=[tmp.tok])
        K.op("dve", lambda e: e.tensor_tensor(msq.ap, mean.ap, mean.ap, ALU.mult), reads=[tmp.tok], writes=[tmp.tok])
        K.op("dve", lambda e: e.scalar_tensor_tensor(msq.ap, bQ.ap[:, 0:TBW], 1.0 / DM, msq.ap, ALU.mult, ALU.subtract),
             reads=[bQ.tok, tmp.tok], writes=[tmp.tok])
        K.op("act", lambda e: e.activation(out=msq.ap, in_=msq.ap, func=AF.Sqrt, bias=LN_EPS), reads=[tmp.tok], writes=[tmp.tok])
        K.op("dve", lambda e: e.reciprocal(rstd.ap, msq.ap), reads=[tmp.tok], writes=[tmp.tok])
        yield
        for oc in range(8):
            K.op("dve", lambda e, oc=oc: e.tensor_tensor(zsq.ap[:, oc, :], z.ap[:, oc, :], mean.ap, ALU.subtract),
                 reads=[z.tok, tmp.tok], writes=[zsq.tok])
            K.op("dve", lambda e, oc=oc: e.tensor_tensor(zsq.ap[:, oc, :], zsq.ap[:, oc, :], rstd.ap, ALU.mult),
                 reads=[zsq.tok, tmp.tok], writes=[zsq.tok])
            st = stg[sti[0] % len(stg)]
            K.op("act", lambda e, oc=oc, st=st: e.activation(out=st.ap[:, 0:TBW], in_=zsq.ap[:, oc, :], func=AF.Identity,
                                                            scale=pcol(l, gname, oc), bias=pcol(l, bname, oc)),
                 reads=[zsq.tok], writes=[st.tok])
            K.dma("sp", dst[oc * 128:(oc + 1) * 128, t0:t0 + TBW], st.ap[:, 0:TBW], reads=[st.tok],
                  sem="st%d" % (sti[0] % len(stg)))
            sti[0] += 1
            yield

    def stage_merge(l, xsrc):
        A.reset()
        wg = A.alloc([8, 4096], BF16)
        wb = A.alloc([16, DM], BF16)
        wo = A.alloc([8, DM], BF16)
        for kc in range(8):
            K.dma("pool", wg.ap[:, kc, :], w_gate[l, kc * 128:(kc + 1) * 128, :], writes=[wg.tok], sem="ld_w")
            K.dma("pool", wo.ap[:, kc, :], w_out[l, kc * 128:(kc + 1) * 128, :], writes=[wo.tok], sem="ld_w")
        for n in range(4):
            for kc in range(4):
                K.dma("pool", wb.ap[:, n * 4 + kc, :], w_branch[l, n, kc * 128:(kc + 1) * 128, :], writes=[wb.tok], sem="ld_w")
        TBW = 256
        xb = [A.alloc([8, TBW], BF16) for _ in range(2)]
        xf = [A.alloc([8, TBW], F32) for _ in range(1)] * 2
        yb = [A.alloc([16, TBW], BF16) for _ in range(2)]
        gs = [A.alloc([TBW], F32) for _ in range(2)]
        tm = [A.alloc([TBW], F32) for _ in range(2)]
        macc = A.alloc([TBW], F32)
        mb = A.alloc([8, TBW], BF16)
        z = A.alloc([8, TBW], F32)
        zsq = A.alloc([8, TBW], F32)
        tmp = A.alloc([3, TBW], F32)
        stg = [A.alloc([TBW], F32) for _ in range(3)]
        sti = [0]
        xs3 = xsrc.rearrange("(kc p) t -> p kc t", p=128)
        ya3 = yall.rearrange("(kc p) t -> p kc t", p=128)
        gic = [0]

        def partA(tb):
            ts = slice(tb * TBW, (tb + 1) * TBW)
            x_b, y_b = xb[tb % 2], yb[tb % 2]
            K.dma("pool", x_b.ap, xs3[:, :, ts], writes=[x_b.tok], sem="ld_x%d" % (tb % 2))
            K.dma("sp", y_b.ap, ya3[:, :, ts], writes=[y_b.tok], sem="ld_y%d" % (tb % 2))
            for oc in range(8):
                for n in range(4):
                    gi = gic[0]
                    bG = bank()
                    bB = bank()
                    for kc in range(8):
                        K.op("pe", lambda e, kc=kc: e.matmul(bG.ap[:, 0:TBW], wg.ap[:, kc, n * 1024 + oc * 128:n * 1024 + (oc + 1) * 128],
                                                             x_b.ap[:, kc, :], start=(kc == 0), stop=(kc == 7)),
                             reads=[wg.tok, x_b.tok], writes=[bG.tok], signal=(kc == 7))
                    for kc in range(4):
                        K.op("pe", lambda e, kc=kc: e.matmul(bB.ap[:, 0:TBW], wb.ap[:, n * 4 + kc, oc * 128:(oc + 1) * 128],
                                                             y_b.ap[:, n * 4 + kc, :], start=(kc == 0), stop=(kc == 3)),
                             reads=[wb.tok, y_b.tok], writes=[bB.tok], signal=(kc == 3))
                    g_ = gs[gi % 2]
                    t_ = tm[gi % 2]
                    gic[0] += 1
                    K.op("act", lambda e: e.activation(out=g_.ap, in_=bG.ap[:, 0:TBW], func=AF.Sigmoid, bias=pcol(l, "bg", n * 8 + oc)),
                         reads=[bG.tok], writes=[g_.tok])
                    if n == 0:
                        K.op("dve", lambda e: e.tensor_tensor(macc.ap, g_.ap, bB.ap[:, 0:TBW], ALU.mult), reads=[g_.tok, bB.tok],
                             writes=[macc.tok])
                    else:
                        K.op("dve", lambda e: e.tensor_tensor(t_.ap, g_.ap, bB.ap[:, 0:TBW], ALU.mult), reads=[g_.tok, bB.tok],
                             writes=[t_.tok])
                        dst = mb.ap[:, oc, :] if n == 3 else macc.ap
                        dtok = mb.tok if n == 3 else macc.tok
                        K.op("dve", lambda e, dst=dst: e.tensor_tensor(dst, macc.ap, t_.ap, ALU.add), reads=[macc.tok, t_.tok],
                             writes=[dtok])
                    yield

        def partB(tb):
            ts = slice(tb * TBW, (tb + 1) * TBW)
            x_f = xf[tb % 2]
            K.dma("sp", x_f.ap, xs3[:, :, ts], writes=[x_f.tok], sem="ld_xf0")
            for oc in range(8):
                bZ = bank()
                for kc in range(8):
                    K.op("pe", lambda e, kc=kc: e.matmul(bZ.ap[:, 0:TBW], wo.ap[:, kc, oc * 128:(oc + 1) * 128], mb.ap[:, kc, :],
                                                         start=(kc == 0), stop=(kc == 7)),
                         reads=[wo.tok, mb.tok], writes=[bZ.tok], signal=(kc == 7))
                K.op("dve", lambda e: e.scalar_tensor_tensor(z.ap[:, oc, :], x_f.ap[:, oc, :], ALPHA, bZ.ap[:, 0:TBW], ALU.mult, ALU.add),
                     reads=[x_f.tok, bZ.tok], writes=[z.tok])

        ntb = T // TBW
        interleave(partA(0))
        for tb in range(ntb):
            partB(tb)
            interleave(layernorm_fm(z, zsq, tmp, l, "l1g", "l1b", TBW, x1d, tb * TBW, stg, sti),
                       partA(tb + 1) if tb + 1 < ntb else None)
        K.barrier()

    def stage_ple(l):
        A.reset()
        TBW = 512
        wp = A.alloc([2, DM], BF16)
        wq = A.alloc([8, DM], BF16)
        for kc in range(8):
            K.dma("pool", wq.ap[:, kc, :], w_pg[l, kc * 128:(kc + 1) * 128, :], writes=[wq.tok], sem="ld_w")
        for kc in range(2):
            K.dma("pool", wp.ap[:, kc, :], w_ple[l, kc * 128:(kc + 1) * 128, :], writes=[wp.tok], sem="ld_w")
        xb = [A.alloc([8, TBW], BF16) for _ in range(2)]
        xf = [A.alloc([8, TBW], F32) for _ in range(2)]
        pb = [A.alloc([2, TBW], BF16) for _ in range(2)]
        sg = [A.alloc([TBW], F32) for _ in range(2)]
        stg = [A.alloc([TBW], F32) for _ in range(3)]
        si = 0
        xs3 = x1d.rearrange("(kc p) t -> p kc t", p=128)
        ps3 = pT[l].rearrange("(kc p) t -> p kc t", p=128)
        for tb in range(T // TBW):
            ts = slice(tb * TBW, (tb + 1) * TBW)
            x_b, x_f, p_b = xb[tb % 2], xf[tb % 2], pb[tb % 2]
            K.dma("pool", x_b.ap, xs3[:, :, ts], writes=[x_b.tok], sem="ld_x%d" % (tb % 2))
            K.dma("sp", x_f.ap, xs3[:, :, ts], writes=[x_f.tok], sem="ld_xf%d" % (tb % 2))
            K.dma("pool", p_b.ap, ps3[:, :, ts], writes=[p_b.tok], sem="ld_y%d" % (tb % 2))
            for oc in range(8):
                bP = bank()
                bQ = bank()
                osl = slice(oc * 128, (oc + 1) * 128)
                for kc in range(2):
                    K.op("pe", lambda e: e.matmul(bP.ap, wp.ap[:, kc, osl], p_b.ap[:, kc, :], start=(kc == 0), stop=(kc == 1)),
                         reads=[wp.tok, p_b.tok], writes=[bP.tok], signal=(kc == 1))
                for kc in range(8):
                    K.op("pe", lambda e: e.matmul(bQ.ap, wq.ap[:, kc, osl], x_b.ap[:, kc, :], start=(kc == 0), stop=(kc == 7)),
                         reads=[wq.tok, x_b.tok], writes=[bQ.tok], signal=(kc == 7))
                s_ = sg[oc % 2]
                st = stg[si % 3]
                K.op("act", lambda e: e.activation(out=s_.ap, in_=bQ.ap, func=AF.Sigmoid, bias=pcol(l, "bpg", oc)),
                     reads=[bQ.tok], writes=[s_.tok])
                K.op("dve", lambda e: e.tensor_tensor(s_.ap, s_.ap, bP.ap, ALU.mult), reads=[s_.tok, bP.tok], writes=[s_.tok])
                K.op("dve", lambda e: e.scalar_tensor_tensor(st.ap, x_f.ap[:, oc, :], ALPHA, s_.ap, ALU.mult, ALU.add),
                     reads=[x_f.tok, s_.tok], writes=[st.tok])
                K.dma("sp", based[oc * 128:(oc + 1) * 128, ts], st.ap, reads=[st.tok], sem="st%d" % (si % 3))
                si += 1
        K.barrier()

    def stage_ffn(l, dst):
        A.reset()
        TBW = 256
        w1 = A.alloc([8, 4096], BF16)
        w2 = A.alloc([32, DM], BF16)
        for kc in range(8):
            K.dma("pool", w1.ap[:, kc, :], w_ff1[l, kc * 128:(kc + 1) * 128, :], writes=[w1.tok], sem="ld_w")
        for kc in range(32):
            K.dma("pool", w2.ap[:, kc, :], w_ff2[l, kc * 128:(kc + 1) * 128, :], writes=[w2.tok], sem="ld_w")
        xb = [A.alloc([8, TBW], BF16) for _ in range(2)]
        hid = A.alloc([32, TBW], BF16)
        zsq = A.alloc([8, TBW], F32)
        rl = [A.alloc([TBW], F32) for _ in range(2)]
        z = A.alloc([8, TBW], F32)
        tmp = A.alloc([3, TBW], F32)
        stg = [A.alloc([TBW], F32) for _ in range(3)]
        sti = [0]
        xs3 = x1d.rearrange("(kc p) t -> p kc t", p=128)
        bs3 = based.rearrange("(kc p) t -> p kc t", p=128)
        ric = [0]

        def ff1(tb):
            ts = slice(tb * TBW, (tb + 1) * TBW)
            x_b = xb[tb % 2]
            K.dma("pool", x_b.ap, xs3[:, :, ts], writes=[x_b.tok], sem="ld_x%d" % (tb % 2))
            for fc in range(32):
                bH = bank()
                for kc in range(8):
                    K.op("pe", lambda e: e.matmul(bH.ap[:, 0:TBW], w1.ap[:, kc, fc * 128:(fc + 1) * 128], x_b.ap[:, kc, :],
                                                  start=(kc == 0), stop=(kc == 7)),
                         reads=[w1.tok, x_b.tok], writes=[bH.tok], signal=(kc == 7))
                r_ = rl[ric[0] % 2]
                ric[0] += 1
                K.op("act", lambda e: e.activation(out=r_.ap, in_=bH.ap[:, 0:TBW], func=AF.Relu), reads=[bH.tok], writes=[r_.tok])
                K.op("dve", lambda e: e.tensor_tensor(hid.ap[:, fc, :], r_.ap, r_.ap, ALU.mult), reads=[r_.tok], writes=[hid.tok])
                yield

        def ff2(tb):
            ts = slice(tb * TBW, (tb + 1) * TBW)
            K.dma("sp", z.ap, bs3[:, :, ts], writes=[z.tok], sem="ld_xf0")
            for oc in range(8):
                bF = bank()
                osl = slice(oc * 128, (oc + 1) * 128)
                for fc in range(32):
                    K.op("pe", lambda e: e.matmul(bF.ap[:, 0:TBW], w2.ap[:, fc, osl], hid.ap[:, fc, :],
                                                  start=(fc == 0), stop=(fc == 31)),
                         reads=[w2.tok, hid.tok], writes=[bF.tok], signal=(fc == 31))
                K.op("dve", lambda e: e.tensor_tensor(z.ap[:, oc, :], z.ap[:, oc, :], bF.ap[:, 0:TBW], ALU.add),
                     reads=[z.tok, bF.tok], writes=[z.tok])

        ntb = T // TBW
        interleave(ff1(0))
        for tb in range(ntb):
            ff2(tb)
            interleave(layernorm_fm(z, zsq, tmp, l, "l2g", "l2b", TBW, dst, tb * TBW, stg, sti),
                       ff1(tb + 1) if tb + 1 < ntb else None)
        K.barrier()

    def pair():
        i = bank_i[0]
        if i % 2:
            i += 1
        bank_i[0] = i + 2
        a, b_ = banks[i % 8], banks[(i + 1) % 8]
        o = (i % 8) * 512
        return psum_t[:, o:o + 1024], [a.tok, b_.tok]

    def stage_rwkv(l):
        A.reset()
        RW = 256
        NCH = RW // 64
        w2s = A.alloc([512], F32)
        a2s = A.alloc([512], F32)
        g2s = A.alloc([512], F32)
        muvb = A.alloc([512], F32)
        wtok = Tok()
        K.dma("sp", w2s.ap[0:64, :], rw2[l], writes=[wtok], sem="ld_w")
        K.dma("sp", a2s.ap[64:128, :], ra2[l], writes=[wtok], sem="ld_w")
        K.dma("sp", g2s.ap, rg2[l], writes=[wtok], sem="ld_w")
        K.dma("sp", muvb.ap[0:64, :], muv[l], writes=[wtok], sem="ld_w")
        f = lambda: A.alloc([RW], F32)
        zwa, zg = A.alloc([RW + 1], F32), A.alloc([RW + 1], F32)
        zr, zk, zv = A.alloc([RW + 1], F32), A.alloc([RW + 1], F32), A.alloc([RW + 1], F32)
        zwap, zgp, dd = f(), f(), f()
        rp, kp, vp = f(), f(), f()
        sgw, av, kk, kkn, k2, bv, cum, cumx, e1, e2, e3, t1_, t2_ = [f() for _ in range(13)]
        vraw = A.alloc([NCH, 512], F32)
        vprev = A.alloc([NCH, 512], F32)

        class BSet:
            pass

        sets = []
        for _ in range(2):
            s_ = BSet()
            s_.gall = A.alloc([4, RW], F32)
            s_.bonus = A.alloc([4, RW], F32)
            s_.AR = A.alloc([8, NCH, 2, 64], BF16)
            s_.KBt = A.alloc([8, NCH, 2, 64], BF16)
            s_.gam = A.alloc([8, NCH], F32)
            s_.VT = A.alloc([NCH, 512], BF16)
            s_.KT = A.alloc([NCH, 512], BF16)
            s_.BT = A.alloc([NCH, 512], BF16)
            s_.AKR = A.alloc([NCH, 8, 128], BF16)
            s_.ABR = A.alloc([NCH, 8, 64], BF16)
            s_.Tbf = A.alloc([NCH, 8, 64], BF16)
            sets.append(s_)
        Ma, Mb, MTa, MTb, Tb16 = [A.alloc([16, 64], BF16) for _ in range(5)]
        T32 = A.alloc([16, 64], F32)
        ctoks = {id(t_): [Tok(), Tok()] for t_ in (Ma, Mb, MTa, MTb, Tb16, T32)}
        WT = A.alloc([512], BF16)
        UT = A.alloc([512], BF16)
        Sf = A.alloc([8, 64], F32)
        Sb = A.alloc([8, 64], BF16)
        tmpS = A.alloc([8, 64], F32)
        Y = A.alloc([NCH, 512], F32)
        Ysq = A.alloc([NCH, 512], F32)
        st1 = A.alloc([NCH * 8], F32)
        st2 = A.alloc([NCH * 8], F32)
        st3 = A.alloc([NCH * 8], F32)
        to1 = A.alloc([RW], F32)
        ostg = [A.alloc([RW], BF16) for _ in range(2)]
        osi = [0]
        P64 = slice(0, 64)
        PS = [slice(0, 64), slice(64, 128)]
        m1b = mask1.ap[0:64, :].unsqueeze(1).to_broadcast([64, 8, 128])
        mSb = mask1.ap[0:64, 0:64].unsqueeze(1).to_broadcast([64, 8, 64])
        mIb = mask1.ap[0:64, 64:128].unsqueeze(1).to_broadcast([64, 8, 64])
        mLb = masksl.ap[0:64, :].unsqueeze(1).to_broadcast([64, 8, 64])
        idb_ = ident.ap[0:64, 0:64].unsqueeze(1).to_broadcast([64, 8, 64])

        def lerp(zt, out, mucol, parts=slice(0, 128)):
            K.op("dve", lambda e: e.tensor_tensor(dd.ap[parts, :], zt.ap[parts, 0:RW], zt.ap[parts, 1:RW + 1], ALU.subtract),
                 reads=[zt.tok], writes=[dd.tok])
            K.op("dve", lambda e: e.scalar_tensor_tensor(out.ap[parts, :], dd.ap[parts, :], mucol, zt.ap[parts, 1:RW + 1],
                                                         ALU.mult, ALU.add), reads=[dd.tok, zt.tok], writes=[out.tok])

        def loadz(zt, row0, b, blk, nm):
            t0 = b * S + blk * RW
            if blk == 0:
                K.op("dve", lambda e: e.memset(zt.ap[:, 0:1], 0.0), writes=[zt.tok])
                K.dma("sp", zt.ap[:, 1:RW + 1], hFM[row0:row0 + 128, t0:t0 + RW], writes=[zt.tok], sem="ld_" + nm)
            else:
                K.dma("sp", zt.ap, hFM[row0:row0 + 128, t0 - 1:t0 + RW], writes=[zt.tok], sem="ld_" + nm)

        def genA(b, blk, st):
            AR, KBt, gam, gall, bonus, VT, KT, BT, AKR, ABR, Tbf = (st.AR, st.KBt, st.gam, st.gall, st.bonus, st.VT, st.KT,
                                                                    st.BT, st.AKR, st.ABR, st.Tbf)
            t0 = b * S + blk * RW
            loadz(zwa, 4096, b, blk, "zwa")
            loadz(zg, 4224, b, blk, "zg")
            lerp(zwa, zwap, pcol(l, "mu", 12))
            lerp(zg, zgp, pcol(l, "mu", 13))
            K.op("act", lambda e: e.activation(out=zwap.ap[0:64, :], in_=zwap.ap[0:64, :], func=AF.Tanh),
                 reads=[zwap.tok], writes=[zwap.tok])
            K.op("act", lambda e: e.activation(out=zgp.ap, in_=zgp.ap, func=AF.Sigmoid), reads=[zgp.tok], writes=[zgp.tok])
            yield
            for hp in range(4):
                hs = slice(hp * 128, (hp + 1) * 128)
                loadz(zr, 2560 + hp * 128, b, blk, "zr")
                loadz(zk, 3072 + hp * 128, b, blk, "zk")
                loadz(zv, 3584 + hp * 128, b, blk, "zv")
                lerp(zr, rp, pcol(l, "mu", hp))
                lerp(zk, kp, pcol(l, "mu", 4 + hp))
                lerp(zv, vp, pcol(l, "mu", 8 + hp))
                bW, bA, bG = bank(), bank(), bank()
                K.op("pe", lambda e: e.matmul(bW.ap[:, 0:RW], w2s.ap[0:64, hs], zwap.ap[0:64, :], start=True, stop=True),
                     reads=[wtok, zwap.tok], writes=[bW.tok])
                K.op("pe", lambda e: e.matmul(bA.ap[:, 0:RW], a2s.ap[64:128, hs], zwap.ap[64:128, :], start=True, stop=True),
                     reads=[wtok, zwap.tok], writes=[bA.tok])
                K.op("pe", lambda e: e.matmul(bG.ap[:, 0:RW], g2s.ap[:, hs], zgp.ap, start=True, stop=True),
                     reads=[wtok, zgp.tok], writes=[bG.tok])
                yield
                K.op("act", lambda e: e.activation(out=sgw.ap, in_=bW.ap[:, 0:RW], func=AF.Sigmoid, bias=pcol(l, "w0", hp)),
                     reads=[bW.tok], writes=[sgw.tok])
                K.op("act", lambda e: e.activation(out=av.ap, in_=bA.ap[:, 0:RW], func=AF.Sigmoid, bias=pcol(l, "a0", hp)),
                     reads=[bA.tok], writes=[av.tok])
                K.op("act", lambda e: e.copy(gall.ap[:, hp, :], bG.ap[:, 0:RW]), reads=[bG.tok], writes=[gall.tok])
                K.op("dve", lambda e: e.tensor_scalar(kk.ap, kp.ap, pcol(l, "kk", hp), None, ALU.mult), reads=[kp.tok],
                     writes=[kk.tok])
                K.op("act", lambda e: e.activation(out=t1_.ap, in_=kk.ap, func=AF.Square), reads=[kk.tok], writes=[t1_.tok])
                bN = bank()
                K.op("pe", lambda e: e.matmul(bN.ap[:, 0:RW], onesbd.ap, t1_.ap, start=True, stop=True), reads=[t1_.tok],
                     writes=[bN.tok])
                yield
                K.op("act", lambda e: e.activation(out=t2_.ap, in_=bN.ap[:, 0:RW], func=AF.Sqrt), reads=[bN.tok],
                     writes=[t2_.tok])
                K.op("dve", lambda e: e.tensor_scalar(t2_.ap, t2_.ap, 1e-12, None, ALU.max), reads=[t2_.tok], writes=[t2_.tok])
                K.op("dve", lambda e: e.reciprocal(t1_.ap, t2_.ap), reads=[t2_.tok, t1_.tok], writes=[t1_.tok])
                K.op("dve", lambda e: e.tensor_tensor(kkn.ap, kk.ap, t1_.ap, ALU.mult), reads=[kk.tok, t1_.tok], writes=[kkn.tok])
                K.op("dve", lambda e: e.tensor_scalar(t2_.ap, av.ap, pcol(l, "ka", hp), pcol(l, "omka", hp), ALU.mult, ALU.add),
                     reads=[av.tok, t2_.tok], writes=[t2_.tok])
                K.op("dve", lambda e: e.tensor_tensor(k2.ap, kp.ap, t2_.ap, ALU.mult), reads=[kp.tok, t2_.tok], writes=[k2.tok])
                K.op("dve", lambda e: e.tensor_tensor(bv.ap, kkn.ap, av.ap, ALU.mult), reads=[kkn.tok, av.tok], writes=[bv.tok])
                K.op("dve", lambda e: e.scalar_tensor_tensor(t1_.ap, rp.ap, pcol(l, "rk", hp), k2.ap, ALU.mult, ALU.mult),
                     reads=[rp.tok, k2.tok, t1_.tok], writes=[t1_.tok])
                bR = bank()
                K.op("pe", lambda e: e.matmul(bR.ap[:, 0:RW], onesbd.ap, t1_.ap, start=True, stop=True), reads=[t1_.tok],
                     writes=[bR.tok])
                yield
                K.op("dve", lambda e: e.tensor_tensor(bonus.ap[:, hp, :], bR.ap[:, 0:RW], vp.ap, ALU.mult),
                     reads=[bR.tok, vp.tok], writes=[bonus.tok])
                K.op("dve", lambda e: e.tensor_tensor_scan(cum.ap, rmask.ap[:, 0:RW], sgw.ap, 0.0, ALU.mult, ALU.add),
                     reads=[sgw.tok], writes=[cum.tok])
                K.op("dve", lambda e: e.tensor_tensor(cumx.ap, cum.ap, sgw.ap, ALU.subtract), reads=[cum.tok, sgw.tok],
                     writes=[cumx.tok])
                K.op("act", lambda e: e.activation(out=e1.ap, in_=cumx.ap, func=AF.Exp, scale=-C0), reads=[cumx.tok], writes=[e1.tok])
                K.op("act", lambda e: e.activation(out=e2.ap, in_=cum.ap, func=AF.Exp, scale=-C0), reads=[cum.tok], writes=[e2.tok])
                K.op("act", lambda e: e.activation(out=e3.ap, in_=cum.ap, func=AF.Exp, scale=C0), reads=[cum.tok], writes=[e3.tok])
                yield
                for hh in range(2):
                    h = 2 * hp + hh
                    pr = slice(hh * 64, hh * 64 + 64)
                    v3 = lambda t_: t_.ap[pr, :].rearrange("p (c j) -> p c j", c=NCH)
                    K.op("dve", lambda e: e.scalar_tensor_tensor(AR.ap[P64, h, :, 0, :], v3(kkn), -1.0, v3(e1), ALU.mult, ALU.mult),
                         reads=[kkn.tok, e1.tok], writes=[AR.tok])
                    K.op("pool" if hh == 0 else "dve", lambda e: e.tensor_tensor(AR.ap[P64, h, :, 1, :], v3(rp), v3(e2), ALU.mult),
                         reads=[rp.tok, e2.tok], writes=[AR.tok])
                    K.op("dve", lambda e: e.tensor_tensor(KBt.ap[P64, h, :, 0, :], v3(k2), v3(e3), ALU.mult),
                         reads=[k2.tok, e3.tok], writes=[KBt.tok])
                    K.op("pool" if hh == 0 else "dve", lambda e: e.tensor_tensor(KBt.ap[P64, h, :, 1, :], v3(bv), v3(e3), ALU.mult),
                         reads=[bv.tok, e3.tok], writes=[KBt.tok])
                    K.op("act", lambda e: e.activation(out=gam.ap[P64, h, :], in_=v3(cum)[:, :, 63], func=AF.Exp, scale=-C0),
                         reads=[cum.tok], writes=[gam.tok])
                yield
            v3d = vTM[t0:t0 + RW, 0:512].rearrange("(c p) f -> p c f", p=64)
            K.dma("sp", vraw.ap[P64], v3d, writes=[vraw.tok], sem="ld_vr")
            if blk == 0:
                K.op("dve", lambda e: e.memset(vprev.ap[0:1, 0, :], 0.0), writes=[vprev.tok])
                K.dma("sp", vprev.ap[1:64, 0, :], vTM[t0:t0 + 63, 0:512], writes=[vprev.tok], sem="ld_vp")
                K.dma("sp", vprev.ap[P64, 1:NCH, :],
                      vTM[t0 + 63:t0 + RW - 1, 0:512].rearrange("(c p) f -> p c f", p=64), writes=[vprev.tok], sem="ld_vp")
            else:
                K.dma("sp", vprev.ap[P64], vTM[t0 - 1:t0 + RW - 1, 0:512].rearrange("(c p) f -> p c f", p=64),
                      writes=[vprev.tok], sem="ld_vp")
            K.op("pool", lambda e: e.tensor_tensor(vprev.ap[P64], vprev.ap[P64], vraw.ap[P64], ALU.subtract),
                 reads=[vraw.tok, vprev.tok], writes=[vprev.tok])
            K.op("pool", lambda e: e.tensor_tensor(vprev.ap[P64], vprev.ap[P64],
                                                   muvb.ap[P64, :].unsqueeze(1).to_broadcast([64, NCH, 512]), ALU.mult),
                 reads=[vprev.tok, wtok], writes=[vprev.tok])
            K.op("pool", lambda e: e.tensor_tensor(VT.ap[P64], vprev.ap[P64], vraw.ap[P64], ALU.add),
                 reads=[vraw.tok, vprev.tok], writes=[VT.tok])
            yield
            for which, dstT in ((0, KT), (1, BT)):
                for cg in range(NCH // 2):
                    bk = bank()
                    pb = bk.ap.bitcast(BF16)
                    for ci in range(2):
                        c = cg * 2 + ci
                        for h in range(8):
                            K.op("pe", lambda e: e.transpose(pb[P64, ci * 512 + h * 64:ci * 512 + (h + 1) * 64],
                                                             KBt.ap[P64, h, c, which, :], identb.ap[0:64, 0:64]),
                                 reads=[KBt.tok], writes=[bk.tok], signal=(ci == 1 and h == 7))
                    K.op(evac_eng(), copy_op("act" if evi[0] % 2 else "dve",
                                             dstT.ap[P64, cg * 2:cg * 2 + 2, :].rearrange("p c f -> p (c f)"), pb[P64, :]),
                         reads=[bk.tok], writes=[dstT.tok])
                    yield
            mSb2 = mask1.ap[64:128, 0:64].unsqueeze(1).to_broadcast([64, 8, 64])
            mIb2 = mask1.ap[64:128, 64:128].unsqueeze(1).to_broadcast([64, 8, 64])
            for c in range(NCH):
                s_i = c // 2
                sl0 = (c % 2) * 8
                psl = PS[s_i]
                pm, pmt = pair()
                b3 = bank()
                for h in range(8):
                    arr = AR.ap[P64, h, c, :, :].rearrange("p a j -> p (a j)")
                    kb2 = KBt.ap[P64, h, c, :, :].rearrange("p a j -> p (a j)")
                    K.op("pe", lambda e: e.matmul(pm[:, h * 128:(h + 1) * 128], kb2, arr, start=True, stop=True),
                         reads=[KBt.tok, AR.tok], writes=pmt, signal=(h == 7))
                for h in range(8):
                    K.op("pe", lambda e: e.matmul(b3.ap[P64, h * 64:(h + 1) * 64], AR.ap[P64, h, c, 0, :], KBt.ap[P64, h, c, 1, :],
                                                  start=True, stop=True),
                         reads=[KBt.tok, AR.tok], writes=[b3.tok], signal=(h == 7))
                yield
                p1v = pm[P64, :].rearrange("p (h k) -> p h k", h=8)
                p2v = pm[64:128, :].rearrange("p (h k) -> p h k", h=8)
                b3v = b3.ap[P64, :].rearrange("p (h k) -> p h k", h=8)
                K.op("dve", lambda e: e.tensor_tensor(AKR.ap[P64, c], p1v, m1b, ALU.mult), reads=pmt, writes=[AKR.tok])
                K.op("dve", lambda e: e.tensor_tensor(Ma.ap[psl, sl0:sl0 + 8, :], p2v[:, :, 0:64], mSb2, ALU.mult), reads=pmt,
                     writes=[ctoks[id(Ma)][s_i]])
                K.op("dve", lambda e: e.tensor_tensor(ABR.ap[P64, c], p2v[:, :, 64:128], mIb2, ALU.mult), reads=pmt,
                     writes=[ABR.tok])
                K.op("dve", lambda e: e.tensor_tensor(MTa.ap[psl, sl0:sl0 + 8, :], b3v, mLb, ALU.mult), reads=[b3.tok],
                     writes=[ctoks[id(MTa)][s_i]])
                idb_s = ident.ap[psl, 64 * s_i:64 * s_i + 64].unsqueeze(1).to_broadcast([64, 8, 64])
                K.op("dve", lambda e: e.tensor_tensor(T32.ap[psl, sl0:sl0 + 8, :], Ma.ap[psl, sl0:sl0 + 8, :], idb_s, ALU.add),
                     reads=[ctoks[id(Ma)][s_i]], writes=[ctoks[id(T32)][s_i]])
                K.op("act", lambda e: e.copy(Tb16.ap[psl, sl0:sl0 + 8, :], T32.ap[psl, sl0:sl0 + 8, :]),
                     reads=[ctoks[id(T32)][s_i]], writes=[ctoks[id(Tb16)][s_i]])
                yield
            cur = [[Ma, MTa], [Ma, MTa]]
            nxt = [[Mb, MTb], [Mb, MTb]]
            tk = lambda t_, s_i: ctoks[id(t_)][s_i]
            for i in range(1, 6):
                bMT = [pair(), pair()]
                bM = [pair(), pair()] if i < 5 else None
                if i < 5:
                    for q in range(16):
                        for s_i in range(2):
                            psl = PS[s_i]
                            M, MT = cur[s_i]
                            K.op("pe", lambda e: e.matmul(bM[s_i][0][psl, q * 64:(q + 1) * 64], MT.ap[psl, q, :], M.ap[psl, q, :],
                                                          start=True, stop=True),
                                 reads=[tk(M, s_i), tk(MT, s_i)], writes=bM[s_i][1], signal=(q == 15))
                for q in range(16):
                    for s_i in range(2):
                        psl = PS[s_i]
                        M, MT = cur[s_i]
                        K.op("pe", lambda e: e.matmul(bMT[s_i][0][psl, q * 64:(q + 1) * 64], M.ap[psl, q, :], MT.ap[psl, q, :],
                                                      start=True, stop=True),
                             reads=[tk(M, s_i), tk(MT, s_i)], writes=bMT[s_i][1], signal=(q == 15))
                for s_i in range(2):
                    psl = PS[s_i]
                    Mn, MTn = nxt[s_i]
                    if i < 5:
                        K.op("act", lambda e: e.copy(Mn.ap[psl].rearrange("p h k -> p (h k)"), bM[s_i][0][psl, :]),
                             reads=bM[s_i][1], writes=[tk(Mn, s_i)])
                    K.op("dve", lambda e: e.tensor_copy(MTn.ap[psl].rearrange("p h k -> p (h k)"), bMT[s_i][0][psl, :]),
                         reads=bMT[s_i][1], writes=[tk(MTn, s_i)])
                yield
                bT = [pair(), pair()]
                for q in range(16):
                    for s_i in range(2):
                        psl = PS[s_i]
                        MTn = nxt[s_i][1]
                        K.op("pe", lambda e: e.matmul(bT[s_i][0][psl, q * 64:(q + 1) * 64], MTn.ap[psl, q, :], Tb16.ap[psl, q, :],
                                                      start=True, stop=True),
                             reads=[tk(MTn, s_i), tk(Tb16, s_i)], writes=bT[s_i][1], signal=(q == 15))
                for s_i in range(2):
                    psl = PS[s_i]
                    t32v = T32.ap[psl].rearrange("p h k -> p (h k)")
                    if i < 5:
                        K.op("dve", lambda e: e.tensor_tensor(t32v, t32v, bT[s_i][0][psl, :], ALU.add),
                             reads=bT[s_i][1] + [tk(T32, s_i)], writes=[tk(T32, s_i)])
                        K.op("act", lambda e: e.copy(Tb16.ap[psl].rearrange("p h k -> p (h k)"), t32v),
                             reads=[tk(T32, s_i)], writes=[tk(Tb16, s_i)])
                    else:
                        K.op("dve", lambda e: e.tensor_tensor(Tbf.ap[P64, 2 * s_i:2 * s_i + 2].rearrange("p c h k -> p (c h k)"),
                                                              t32v, bT[s_i][0][psl, :], ALU.add),
                             reads=bT[s_i][1] + [tk(T32, s_i)], writes=[Tbf.tok])
                    cur[s_i], nxt[s_i] = nxt[s_i], cur[s_i]
                yield

        def genB(b, blk, st):
            AR, KBt, gam, gall, bonus, VT, KT, BT, AKR, ABR, Tbf = (st.AR, st.KBt, st.gam, st.gall, st.bonus, st.VT, st.KT,
                                                                    st.BT, st.AKR, st.ABR, st.Tbf)
            t0 = b * S + blk * RW
            if blk == 0:
                K.op("dve", lambda e: e.memset(Sf.ap[P64], 0.0), writes=[Sf.tok])
                K.op("dve", lambda e: e.memset(Sb.ap[P64], 0.0), writes=[Sb.tok])
            for c in range(NCH):
                bW = bank()
                for h in range(8):
                    hsl = slice(h * 64, (h + 1) * 64)
                    K.op("pe", lambda e: e.matmul(bW.ap[P64, hsl], AR.ap[P64, h, c, 0, :], Sb.ap[P64, h, :], start=True, stop=False),
                         reads=[AR.tok, Sb.tok], writes=[bW.tok], signal=False)
                    K.op("pe", lambda e: e.matmul(bW.ap[P64, hsl], AKR.ap[P64, c, h, 0:64], VT.ap[P64, c, hsl], start=False, stop=True),
                         reads=[AKR.tok, VT.tok], writes=[bW.tok], signal=(h == 7))
                yield
                K.op("act", lambda e: e.copy(WT.ap[P64, :], bW.ap[P64, :]), reads=[bW.tok], writes=[WT.tok])
                bU = bank()
                for h in range(8):
                    hsl = slice(h * 64, (h + 1) * 64)
                    K.op("pe", lambda e: e.matmul(bU.ap[P64, hsl], Tbf.ap[P64, c, h, :], WT.ap[P64, hsl], start=True, stop=True),
                         reads=[Tbf.tok, WT.tok], writes=[bU.tok], signal=(h == 7))
                yield
                K.op("dve", lambda e: e.tensor_copy(UT.ap[P64, :], bU.ap[P64, :]), reads=[bU.tok], writes=[UT.tok])
                bS = bank()
                for h in range(8):
                    hsl = slice(h * 64, (h + 1) * 64)
                    K.op("pe", lambda e: e.matmul(bS.ap[P64, hsl], KT.ap[P64, c, hsl], VT.ap[P64, c, hsl], start=True, stop=False),
                         reads=[KT.tok, VT.tok], writes=[bS.tok], signal=False)
                    K.op("pe", lambda e: e.matmul(bS.ap[P64, hsl], BT.ap[P64, c, hsl], UT.ap[P64, hsl], start=False, stop=True),
                         reads=[BT.tok, UT.tok], writes=[bS.tok], signal=(h == 7))
                bY = bank()
                for h in range(8):
                    hsl = slice(h * 64, (h + 1) * 64)
                    K.op("pe", lambda e: e.matmul(bY.ap[P64, hsl], AR.ap[P64, h, c, 1, :], Sb.ap[P64, h, :], start=True, stop=False),
                         reads=[AR.tok, Sb.tok], writes=[bY.tok], signal=False)
                    K.op("pe", lambda e: e.matmul(bY.ap[P64, hsl], ABR.ap[P64, c, h, :], UT.ap[P64, hsl], start=False, stop=False),
                         reads=[ABR.tok, UT.tok], writes=[bY.tok], signal=False)
                    K.op("pe", lambda e: e.matmul(bY.ap[P64, hsl], AKR.ap[P64, c, h, 64:128], VT.ap[P64, c, hsl], start=False, stop=True),
                         reads=[AKR.tok, VT.tok], writes=[bY.tok], signal=(h == 7))
                yield
                K.op("dve", lambda e: e.tensor_tensor(tmpS.ap[P64].rearrange("p h k -> p (h k)"), bS.ap[P64, :],
                                                      Sf.ap[P64].rearrange("p h k -> p (h k)"), ALU.add),
                     reads=[bS.tok, Sf.tok], writes=[tmpS.tok])
                K.op("dve", lambda e: e.tensor_tensor(Sf.ap[P64], tmpS.ap[P64], gam.ap[P64, :, c:c + 1].to_broadcast([64, 8, 64]),
                                                      ALU.mult), reads=[tmpS.tok, gam.tok], writes=[Sf.tok])
                K.op("act", lambda e: e.copy(Sb.ap[P64], Sf.ap[P64]), reads=[Sf.tok], writes=[Sb.tok])
                K.op("act", lambda e: e.copy(Y.ap[P64, c, :], bY.ap[P64, :]), reads=[bY.tok], writes=[Y.tok])
                yield
            Yv = Y.ap[P64].rearrange("p c (h v) -> p (c h) v", h=8)
            Ysv = Ysq.ap[P64].rearrange("p c (h v) -> p (c h) v", h=8)
            K.op("dve", lambda e: e.tensor_reduce(st1.ap[P64, :], Yv, AX.X, ALU.add), reads=[Y.tok], writes=[st1.tok])
            K.op("act", lambda e: e.activation(out=Ysq.ap[P64], in_=Y.ap[P64], func=AF.Square), reads=[Y.tok], writes=[Ysq.tok])
            yield
            K.op("dve", lambda e: e.tensor_reduce(st2.ap[P64, :], Ysv, AX.X, ALU.add), reads=[Ysq.tok], writes=[st2.tok])
            K.op("dve", lambda e: e.tensor_scalar(st1.ap[P64, :], st1.ap[P64, :], 1.0 / 64, None, ALU.mult), reads=[st1.tok],
                 writes=[st1.tok])
            K.op("dve", lambda e: e.tensor_tensor(st3.ap[P64, :], st1.ap[P64, :], st1.ap[P64, :], ALU.mult), reads=[st1.tok],
                 writes=[st3.tok])
            K.op("dve", lambda e: e.scalar_tensor_tensor(st2.ap[P64, :], st2.ap[P64, :], 1.0 / 64, st3.ap[P64, :], ALU.mult,
                                                         ALU.subtract), reads=[st2.tok, st3.tok], writes=[st2.tok])
            K.op("act", lambda e: e.activation(out=st2.ap[P64, :], in_=st2.ap[P64, :], func=AF.Sqrt, bias=GN_EPS),
                 reads=[st2.tok], writes=[st2.tok])
            K.op("dve", lambda e: e.reciprocal(st3.ap[P64, :], st2.ap[P64, :]), reads=[st2.tok, st3.tok], writes=[st3.tok])
            yield
            K.op("dve", lambda e: e.tensor_tensor(Yv, Yv, st1.ap[P64, :].unsqueeze(2).to_broadcast([64, NCH * 8, 64]), ALU.subtract),
                 reads=[Y.tok, st1.tok], writes=[Y.tok])
            K.op("pool", lambda e: e.tensor_tensor(Yv, Yv, st3.ap[P64, :].unsqueeze(2).to_broadcast([64, NCH * 8, 64]), ALU.mult),
                 reads=[Y.tok, st3.tok], writes=[Y.tok])
            yield
            for hp in range(4):
                bk = bank()
                for c in range(NCH):
                    K.op("pe", lambda e: e.transpose(bk.ap[:, c * 64:(c + 1) * 64], Y.ap[P64, c, hp * 128:(hp + 1) * 128],
                                                     ident.ap[0:64, 0:64]),
                         reads=[Y.tok], writes=[bk.tok], signal=(c == NCH - 1))
                o_ = ostg[osi[0] % 2]
                yield
                K.op("dve", lambda e: e.tensor_scalar(to1.ap, bk.ap[:, 0:RW], pcol(l, "gng", hp), pcol(l, "gnb", hp),
                                                      ALU.mult, ALU.add), reads=[bk.tok], writes=[to1.tok])
                K.op("pool", lambda e: e.tensor_tensor(to1.ap, to1.ap, bonus.ap[:, hp, :], ALU.add), reads=[to1.tok, bonus.tok],
                     writes=[to1.tok])
                K.op("pool", lambda e: e.tensor_tensor(o_.ap, to1.ap, gall.ap[:, hp, :], ALU.mult), reads=[to1.tok, gall.tok],
                     writes=[o_.tok])
                K.dma("sp", yall[1024 + hp * 128:1024 + (hp + 1) * 128, t0:t0 + RW], o_.ap, reads=[o_.tok],
                      sem="st%d" % (osi[0] % 2))
                osi[0] += 1
                yield

        blocks = [(b, blk) for b in range(NB) for blk in range(S // RW)]
        prev = None
        for i, (b, blk) in enumerate(blocks):
            st = sets[i % 2]
            gB = genB(prev[0], prev[1], prev[2]) if prev is not None else None
            interleave(genA(b, blk, st), gB)
            prev = (b, blk, st)
        interleave(genB(prev[0], prev[1], prev[2]))
        K.barrier()

    STAGE_RWKV = [stage_rwkv]

    def scoped(name, fn, *a):
        with nc.named_scope(name):
            fn(*a)

    def run():
        if stop_after == ("pre", 0):
            return
        for l in range(nlayers):
            xsrc = xT if l == 0 else xLd
            scoped('inproj%d' % l, stage_inproj, l, xsrc)
            if stop_after == ("inproj", l):
                return
            import os
            if os.environ.get("KDBG_SKIP"):
                stage_merge(l, xsrc)
                stage_ple(l)
                stage_ffn(l, outT if l == nlayers - 1 else xLd)
                continue
            scoped('rglru%d' % l, stage_rglru, l)
            scoped('sconv%d' % l, stage_sconv, l)
            if stop_after == ("ab", l):
                return
            scoped('attn%d' % l, stage_attn, l)
            if stop_after == ("attn", l):
                return
            scoped('rwkv%d' % l, STAGE_RWKV[0], l)
            if stop_after == ("rwkv", l):
                return
            scoped('merge%d' % l, stage_merge, l, xsrc)
            if stop_after == ("merge", l):
                return
            scoped('ple%d' % l, stage_ple, l)
            scoped('ffn%d' % l, stage_ffn, l, outT if l == nlayers - 1 else xLd)

    run()
    K.barrier()
    print('[kernel] instructions:', K.ninstr, 'sems:', len(K.sems))
    es.close()
    return nc


def _consts():
    c = {}
    c["c_ident"] = np.eye(128, dtype=np.float32)
    ob = np.zeros((128, 128), np.float32)
    ob[0:64, 0:64] = 1.0
    ob[64:128, 64:128] = 1.0
    c["c_onesbd"] = ob
    j = np.arange(64)[:, None]
    s = np.arange(64)[None, :]
    c["c_mask1"] = np.concatenate([(j < s), (j <= s)], axis=1).astype(np.float32)
    c["c_masksl"] = (s < j).astype(np.float32)
    rm = np.ones((128, 512), np.float32)
    rm[:, 0::64] = 0.0
    c["c_rmask"] = rm
    return c


def _bias_table(rel_bias):
    ki = np.arange(128)[:, None]
    qi = np.arange(128)[None, :]
    tab = np.empty((128, 8, 5, 128), np.float32)
    for idx in range(5):
        rel = (4 - idx) * 128 + qi - ki
        g = rel_bias[:, np.clip(rel, -128, 128) + 128]
        if idx == 4:
            g = np.where(((ki >= 64) & (qi < 64))[None], np.float32(-30000.0), g)
        if idx == 0:
            g = np.where(((ki < 64) & (qi >= 64))[None], np.float32(-30000.0), g)
        tab[:, :, idx, :] = g.transpose(1, 0, 2)
    return np.ascontiguousarray(tab.reshape(128, 8 * 5 * 128))


def _pvec(inp):
    out = np.zeros((2, 128, NPV), np.float32)
    for l in range(2):
        P = out[l]

        def put(name, flat):
            n = flat.size // 128
            P[:, PV[name]:PV[name] + n] = flat.reshape(n, 128).T

        P[:, PV["cw"]:PV["cw"] + 16] = inp["lru_conv_w"][l].T.reshape(4, 128, 4).transpose(1, 0, 2).reshape(128, 16)
        put("cb", inp["lru_conv_b"][l]); put("br", inp["lru_br"][l]); put("bi", inp["lru_bi"][l])
        put("lam", inp["lru_lambda"][l])
        P[:, PV["sw"]:PV["sw"] + 12] = inp["sconv_w"][l].T.reshape(4, 128, 3).transpose(1, 0, 2).reshape(128, 12)
        put("mu", inp["rwkv_mu"][l]); put("w0", inp["rwkv_w0"][l]); put("a0", inp["rwkv_a0"][l])
        put("kk", inp["rwkv_k_k"][l]); put("ka", inp["rwkv_k_a"][l]); put("rk", inp["rwkv_r_k"][l].reshape(-1))
        put("gng", inp["rwkv_gn_g"][l]); put("gnb", inp["rwkv_gn_b"][l])
        put("bg", inp["b_gate"][l].reshape(-1)); put("l1g", inp["ln1_g"][l]); put("l1b", inp["ln1_b"][l])
        put("bpg", inp["b_ple_gate"][l]); put("l2g", inp["ln2_g"][l]); put("l2b", inp["ln2_b"][l])
    return out


def _shared_inputs(inp):
    f = lambda a: np.ascontiguousarray(np.asarray(a, dtype=np.float32))
    sh = {
        "w_in": f(inp["w_in"]),
        "w_gate": f(np.asarray(inp["w_gate"]).transpose(0, 2, 1, 3).reshape(2, DM, 4096)),
        "w_branch": f(inp["w_branch"]), "w_out": f(inp["w_out"]), "w_ff1": f(inp["w_ff1"]), "w_ff2": f(inp["w_ff2"]),
        "w_ple": f(inp["w_ple"]), "w_pg": f(inp["w_ple_gate"]), "lru_wr": f(inp["lru_wr"]), "lru_wi": f(inp["lru_wi"]),
        "rw2": f(inp["rwkv_w2"]), "ra2": f(inp["rwkv_a2"]), "rg2": f(inp["rwkv_g2"]),
        "pvec": _pvec({k: np.asarray(v, dtype=np.float32) for k, v in inp.items()}),
        "muv": f(np.broadcast_to(np.asarray(inp["rwkv_mu"], dtype=np.float32)[:, None, 1024:1536], (2, 64, 512))),
        "c_bias": _bias_table(np.asarray(inp["rel_bias"], dtype=np.float32)),
    }
    sh.update(_consts())
    return sh


def _core_inputs(inp, b0, nb):
    x = np.asarray(inp["x"], dtype=np.float32)[b0:b0 + nb]
    p = np.asarray(inp["p"], dtype=np.float32)[:, b0:b0 + nb]
    xT = np.ascontiguousarray(x.reshape(nb * S, DM).T)
    pT = np.ascontiguousarray(p.reshape(2, nb * S, 256).transpose(0, 2, 1))
    return {"xT": xT, "pT": pT}


def kernel(**inputs):
    ncores = 8
    nb = 4
    nc = build_program(nb)
    sh = _shared_inputs(inputs)
    in_maps = []
    for c in range(ncores):
        m = dict(sh)
        m.update(_core_inputs(inputs, c * nb, nb))
        in_maps.append(m)
    res = run_bass_kernel_spmd(nc, in_maps, core_ids=list(range(ncores)))
    out = np.empty((ncores * nb, S, DM), np.float32)
    for c in range(ncores):
        oT = res.results[c]["outT"]
        out[c * nb:(c + 1) * nb] = oT.T.reshape(nb, S, DM)
    return out
```
